# Optimizing a Trainium2 kernel written in Bass

```python
import jax, jax.numpy as jnp
from jax import lax
import numpy as np

D_MODEL = 1024
BATCH = 8
SEQ = 4096
DEPTH = 2

CHUNK = 64
Q_BLOCK = 128
HEAD_DIM = 64
H_A = D_MODEL // 128
H_B = D_MODEL // 128
G_B = 2
HPG_B = H_B // G_B
H_I = D_MODEL // 128
D_I = 32
TOPK_MAX = 256
ROPE_THETA = 500000.0
ROT_DIM = HEAD_DIM // 4
ROT_DIM_I = D_I // 4
D_FF = 4 * D_MODEL
RMS_EPS = 1e-6

W_QA = H_A * HEAD_DIM
W_KA = H_A * HEAD_DIM
W_VA = H_A * HEAD_DIM
W_QB = H_B * HEAD_DIM
W_KB = G_B * HEAD_DIM
W_VB = G_B * HEAD_DIM
W_QI = H_I * D_I
W_KI = D_I
W_WI = H_I
W_GATE = 2 * D_MODEL
N_IN = W_QA + W_KA + W_VA + W_QB + W_KB + W_VB + W_QI + W_KI + W_WI + W_GATE
SPLITS = (W_QA,
          W_QA + W_KA,
          W_QA + W_KA + W_VA,
          W_QA + W_KA + W_VA + W_QB,
          W_QA + W_KA + W_VA + W_QB + W_KB,
          W_QA + W_KA + W_VA + W_QB + W_KB + W_VB,
          W_QA + W_KA + W_VA + W_QB + W_KB + W_VB + W_QI,
          W_QA + W_KA + W_VA + W_QB + W_KB + W_VB + W_QI + W_KI,
          W_QA + W_KA + W_VA + W_QB + W_KB + W_VB + W_QI + W_KI + W_WI)
MIX_A = H_A * HEAD_DIM
MIX_B = H_B * HEAD_DIM

kernel_name = "hybrid_stickbreak_dsa_gated_block"


def rms_norm(x, g):
    xf = x.astype(jnp.float32)
    y = xf * lax.rsqrt(jnp.mean(xf * xf, axis=-1, keepdims=True) + RMS_EPS)
    return (y * g.astype(jnp.float32)).astype(x.dtype)


def rope_tables(seq, rot_dim):
    inv = ROPE_THETA ** (-(jnp.arange(0, rot_dim, 2, dtype=jnp.float32) / rot_dim))
    ang = jnp.arange(seq, dtype=jnp.float32)[:, None] * inv[None, :]
    return jnp.cos(ang), jnp.sin(ang)


def apply_partial_rope(x, cos, sin):
    r = cos.shape[-1] * 2
    xr = x[..., :r].astype(jnp.float32)
    x1, x2 = xr[..., : r // 2], xr[..., r // 2:]
    c = cos[None, :, None, :]
    s = sin[None, :, None, :]
    rot = jnp.concatenate([x1 * c - x2 * s, x2 * c + x1 * s], axis=-1)
    return jnp.concatenate([rot.astype(x.dtype), x[..., r:]], axis=-1)


def to_blocks(a):
    b, s = a.shape[0], a.shape[1]
    return jnp.moveaxis(a.reshape((b, s // Q_BLOCK, Q_BLOCK) + a.shape[2:]), 1, 0)


def from_blocks(a):
    a = jnp.moveaxis(a, 0, 1)
    return a.reshape((a.shape[0], a.shape[1] * a.shape[2]) + a.shape[3:])


def stick_breaking_attention(q, k, v):
    b, s, h, d = q.shape
    scale = d ** -0.5
    key_pos = jnp.arange(s)

    def block(args):
        qb, bi = args
        z = jnp.einsum("bqhd,bkhd->bhqk", qb, k).astype(jnp.float32) * scale
        q_pos = bi * Q_BLOCK + jnp.arange(Q_BLOCK)
        causal = key_pos[None, :] < q_pos[:, None]
        log_beta = jax.nn.log_sigmoid(z)
        log_1m = jnp.where(causal, jax.nn.log_sigmoid(-z), 0.0)
        tail = lax.cumsum(log_1m, axis=3, reverse=True) - log_1m
        log_a = jnp.where(causal, log_beta + tail, -jnp.inf)
        a = jnp.exp(log_a)
        return jnp.einsum("bhqk,bkhd->bqhd", a.astype(v.dtype), v)

    nb = s // Q_BLOCK
    out = lax.map(block, (to_blocks(q), jnp.arange(nb)))
    return from_blocks(out).reshape(b, s, h * d)


def dsa_sparse_attention(q, k, v, q_idx, k_idx, w_idx):
    b, s, g, hg, d = q.shape
    n_keep = min(TOPK_MAX, s // 4)
    scale = d ** -0.5
    key_chunk = jnp.arange(s) // CHUNK
    w_scaled = w_idx.astype(jnp.float32) * (H_I ** -0.5 * D_I ** -0.5)

    def block(args):
        qb, qib, wb, bi = args
        q_chunk = (bi * Q_BLOCK + jnp.arange(Q_BLOCK)) // CHUNK
        rel = jax.nn.relu(jnp.einsum("bqhd,bkd->bqhk", qib, k_idx).astype(jnp.float32))
        iscore = jnp.einsum("bqh,bqhk->bqk", wb, rel)
        admissible = key_chunk[None, :] <= q_chunk[:, None]
        iscore = jnp.where(admissible[None], iscore, -jnp.inf)
        _, idx = lax.top_k(iscore, n_keep)
        kg = jax.vmap(lambda kb, ib: kb[ib])(k, idx)
        vg = jax.vmap(lambda vb, ib: vb[ib])(v, idx)
        valid = (idx // CHUNK) <= q_chunk[None, :, None]
        logits = jnp.einsum("bqghd,bqkgd->bqghk", qb, kg).astype(jnp.float32) * scale
        logits = jnp.where(valid[:, :, None, None, :], logits, -jnp.inf)
        p = jax.nn.softmax(logits, axis=-1)
        return jnp.einsum("bqghk,bqkgd->bqghd", p.astype(vg.dtype), vg)

    nb = s // Q_BLOCK
    out = lax.map(block, (to_blocks(q), to_blocks(q_idx), to_blocks(w_scaled), jnp.arange(nb)))
    return from_blocks(out).reshape(b, s, g * hg * d)


def setup_inputs(seed: int = 0) -> dict:
    key = jax.random.key(seed)
    ks = jax.random.split(key, 11)
    f32 = jnp.float32
    x = jax.random.normal(ks[0], (BATCH, SEQ, D_MODEL), f32)
    g_mix = 1.0 + 0.1 * jax.random.normal(ks[1], (DEPTH, D_MODEL), f32)
    w_in = jax.random.normal(ks[2], (DEPTH, D_MODEL, N_IN), f32) * D_MODEL ** -0.5
    w_up_a = jax.random.normal(ks[3], (DEPTH, MIX_A, D_MODEL), f32) * MIX_A ** -0.5
    w_up_b = jax.random.normal(ks[4], (DEPTH, MIX_B, D_MODEL), f32) * MIX_B ** -0.5
    w_o = jax.random.normal(ks[5], (DEPTH, D_MODEL, D_MODEL), f32) * D_MODEL ** -0.5
    g_mlp = 1.0 + 0.1 * jax.random.normal(ks[6], (DEPTH, D_MODEL), f32)
    w_ff1 = jax.random.normal(ks[7], (DEPTH, D_MODEL, D_FF), f32) * D_MODEL ** -0.5
    w_ff2 = jax.random.normal(ks[8], (DEPTH, D_FF, D_MODEL), f32) * D_FF ** -0.5
    g_final = 1.0 + 0.1 * jax.random.normal(ks[9], (D_MODEL,), f32)
    return {"x": x, "g_mix": g_mix, "w_in": w_in, "w_up_a": w_up_a, "w_up_b": w_up_b,
            "w_o": w_o, "g_mlp": g_mlp, "w_ff1": w_ff1, "w_ff2": w_ff2, "g_final": g_final}


def reference(x, g_mix, w_in, w_up_a, w_up_b, w_o, g_mlp, w_ff1, w_ff2, g_final):
    b, s, _ = x.shape
    cos_b, sin_b = rope_tables(s, ROT_DIM)
    cos_i, sin_i = rope_tables(s, ROT_DIM_I)
    for l in range(DEPTH):
        h = rms_norm(x, g_mix[l])
        proj = h @ w_in[l]
        qa, ka, va, qb, kb, vb, qi, ki, wi, gates = jnp.split(proj, SPLITS, axis=-1)
        qa = qa.reshape(b, s, H_A, HEAD_DIM)
        ka = ka.reshape(b, s, H_A, HEAD_DIM)
        va = va.reshape(b, s, H_A, HEAD_DIM)
        y_a = stick_breaking_attention(qa, ka, va)
        qb = apply_partial_rope(qb.reshape(b, s, H_B, HEAD_DIM), cos_b, sin_b)
        qb = qb.reshape(b, s, G_B, HPG_B, HEAD_DIM)
        kb = apply_partial_rope(kb.reshape(b, s, G_B, HEAD_DIM), cos_b, sin_b)
        vb = vb.reshape(b, s, G_B, HEAD_DIM)
        qi = apply_partial_rope(qi.reshape(b, s, H_I, D_I), cos_i, sin_i)
        ki = apply_partial_rope(ki.reshape(b, s, 1, D_I), cos_i, sin_i)[:, :, 0, :]
        y_b = dsa_sparse_attention(qb, kb, vb, qi, ki, wi)
        gate_a, gate_b = jnp.split(jax.nn.sigmoid(gates), 2, axis=-1)
        merged = gate_a * (y_a @ w_up_a[l]) + gate_b * (y_b @ w_up_b[l])
        x = x + merged @ w_o[l]
        h2 = rms_norm(x, g_mlp[l])
        x = x + jnp.square(jax.nn.relu(h2 @ w_ff1[l])) @ w_ff2[l]
    return rms_norm(x, g_final)
```

```python
import math
from contextlib import ExitStack

import numpy as np
import concourse.bass as bass
import concourse.mybir as mybir
from concourse.bass_utils import run_bass_kernel_spmd

F32 = mybir.dt.float32
BF16 = mybir.dt.bfloat16
AF = mybir.ActivationFunctionType
ALU = mybir.AluOpType
AX = mybir.AxisListType

S = 4096
D = 1024
NT = S // 128
DEPTH = 2
N_IN = 4648
DFF = 4096
EPS = 1e-6
THETA = 500000.0
KBIS = 15
NKEEP = 256

C_QA, C_KA, C_VA, C_QB, C_KB, C_VB, C_QI, C_KI, C_WI, C_G = 0, 512, 1024, 1536, 2048, 2176, 2304, 2560, 2592, 2600
X_QB, X_KB, X_QI, X_KI = 4648, 4648 + 512, 4648 + 640, 4648 + 896
NX = 4648 + 928

DBG = {"stop_after": None, "dump": False}


class Buf:
    __slots__ = ("name", "writer", "rd_eng", "rd_dma")

    def __init__(self, name=""):
        self.name = name
        self.writer = None
        self.rd_eng = {}
        self.rd_dma = []

    def reset(self):
        self.writer = None
        self.rd_eng = {}
        self.rd_dma = []


class Op:
    __slots__ = ("eng", "fn", "deps", "is_dma", "need_inc", "sem", "val", "seq")


ENGS = ["pe", "act", "dve", "pool", "sp"]
NDMA = 24
NDMA_HW = 16


class Sched:
    def __init__(self, nc, block):
        self.nc = nc
        self.block = block
        self.h = {"pe": nc.tensor, "act": nc.scalar, "dve": nc.vector, "pool": nc.gpsimd, "sp": nc.sync}
        self.bstart = {"pe": block.tensor, "act": block.scalar, "dve": block.vector, "pool": block.gpsimd,
                       "sp": block.sync}
        self.psem = {e: nc.alloc_semaphore(f"prog_{e}") for e in ENGS}
        self.pcnt = {e: 0 for e in ENGS}
        self.dsem = [nc.alloc_semaphore(f"dma_{k}") for k in range(NDMA)]
        self.dval = [0] * NDMA
        self.dlast = [None] * NDMA
        self.drr = 0
        self.drr_sw = 0
        self.ops = {e: [] for e in ENGS}
        self.waited = {e: {} for e in ENGS}
        self.bufs = []
        self.seq = 0
        self.last = {e: None for e in ENGS}
        self.phase_dma = []

    def buf(self, name=""):
        b = Buf(name)
        self.bufs.append(b)
        return b

    def bufs_n(self, n, name=""):
        return [self.buf(f"{name}{i}") for i in range(n)]

    def op(self, eng, fn, reads=(), writes=(), dma=False):
        o = Op()
        o.eng, o.fn, o.is_dma, o.need_inc, o.sem, o.val = eng, fn, dma, dma, None, 0
        o.seq = self.seq
        self.seq += 1
        deps = set()
        for b in reads:
            if b.writer is not None:
                deps.add(b.writer)
        for b in writes:
            if b.writer is not None:
                deps.add(b.writer)
            deps.update(b.rd_eng.values())
            deps.update(b.rd_dma)
        if dma:
            if eng == "pool":
                k = NDMA_HW + self.drr_sw
                self.drr_sw = (self.drr_sw + 1) % (NDMA - NDMA_HW)
            else:
                k = self.drr
                self.drr = (self.drr + 1) % NDMA_HW
            if self.dlast[k] is not None:
                deps.add(self.dlast[k])
            self.dval[k] += 16
            o.sem, o.val = self.dsem[k], self.dval[k]
            self.dlast[k] = o
            self.phase_dma.append(o)
        deps.discard(o)
        o.deps = deps
        for b in reads:
            if dma:
                b.rd_dma.append(o)
            else:
                b.rd_eng[eng] = o
        for b in writes:
            b.writer = o
            b.rd_eng = {}
            b.rd_dma = []
        self.ops[eng].append(o)
        if not dma:
            self.last[eng] = o
        return o

    def barrier(self):
        deps = set(x for x in self.last.values() if x is not None)
        deps.update(x for x in self.dlast if x is not None)
        for e in ENGS:
            o = Op()
            o.eng, o.fn, o.is_dma, o.need_inc, o.sem, o.val = e, None, False, False, None, 0
            o.seq = self.seq
            self.seq += 1
            o.deps = set(deps)
            self.ops[e].append(o)

    def flush(self):
        self.barrier()
        for e in ENGS:
            for o in self.ops[e]:
                for d in o.deps:
                    if d.is_dma:
                        continue
                    if d.eng == "pe" and o.eng == "pe" and not o.is_dma:
                        continue
                    d.need_inc = True
        for e in ENGS:
            for o in self.ops[e]:
                if (not o.is_dma) and o.need_inc:
                    self.pcnt[e] += 1
                    o.sem, o.val = self.psem[e], self.pcnt[e]
        for e in ENGS:
            ops = self.ops[e]
            if not ops:
                continue
            waited = self.waited[e]

            def body(eng, ops=ops, waited=waited, e=e):
                for o in ops:
                    need = {}
                    for d in o.deps:
                        if (not d.is_dma) and d.eng == "pe" and e == "pe" and not o.is_dma:
                            continue
                        s = d.sem
                        key = id(s)
                        if key not in need or need[key][1] < d.val:
                            need[key] = (s, d.val)
                    for key, (s, v) in need.items():
                        if waited.get(key, 0) >= v:
                            continue
                        eng.wait_ge(s, v)
                        waited[key] = v
                    if o.fn is None:
                        continue
                    ins = o.fn(eng)
                    if o.is_dma:
                        ins.then_inc(o.sem, 16)
                    elif o.need_inc:
                        ins.then_inc(o.sem, 1)

            self.bstart[e](body)
        self.ops = {e: [] for e in ENGS}
        for b in self.bufs:
            b.reset()
        self.dlast = [None] * NDMA
        self.last = {e: None for e in ENGS}
        self.phase_dma = []


def op_mm(sch, out, lhsT, rhs, start, stop, reads, writes):
    return sch.op("pe", lambda e: e.matmul(out, lhsT=lhsT, rhs=rhs, start=start, stop=stop), reads, writes)


def op_tr(sch, out, in_, ident, reads, writes):
    return sch.op("pe", lambda e: e.transpose(out, in_, ident), reads, writes)


def op_dma(sch, eng, out, in_, reads, writes, **kw):
    return sch.op(eng, lambda e: e.dma_start(out=out, in_=in_, **kw), reads, writes, dma=True)


def op_act(sch, out, in_, func, reads, writes, **kw):
    return sch.op("act", lambda e: e.activation(out=out, in_=in_, func=func, **kw), reads, writes)


def op_ts(sch, eng, out, in0, s1, s2, op0, op1, reads, writes, accum_out=None):
    if op1 is None:
        return sch.op(eng, lambda e: e.tensor_scalar(out=out, in0=in0, scalar1=s1, scalar2=None, op0=op0), reads,
                      writes)
    if accum_out is None:
        return sch.op(eng, lambda e: e.tensor_scalar(out=out, in0=in0, scalar1=s1, scalar2=s2, op0=op0, op1=op1),
                      reads, writes)
    return sch.op(eng, lambda e: e.tensor_scalar(out=out, in0=in0, scalar1=s1, scalar2=s2, op0=op0, op1=op1,
                                                  accum_out=accum_out), reads, writes)


def op_tt(sch, eng, out, in0, in1, op, reads, writes):
    return sch.op(eng, lambda e: e.tensor_tensor(out=out, in0=in0, in1=in1, op=op), reads, writes)


def op_stt(sch, out, in0, scalar, in1, op0, op1, reads, writes, accum_out=None):
    if accum_out is None:
        return sch.op("dve", lambda e: e.scalar_tensor_tensor(out=out, in0=in0, scalar=scalar, in1=in1, op0=op0,
                                                              op1=op1), reads, writes)
    return sch.op("dve", lambda e: e.scalar_tensor_tensor(out=out, in0=in0, scalar=scalar, in1=in1, op0=op0,
                                                          op1=op1, accum_out=accum_out), reads, writes)


def op_copy(sch, eng, out, in_, reads, writes):
    if eng == "act":
        return sch.op("act", lambda e: e.copy(out=out, in_=in_), reads, writes)
    return sch.op(eng, lambda e: e.tensor_copy(out=out, in_=in_), reads, writes)


def op_memset(sch, eng, ap, val, writes):
    return sch.op(eng, lambda e: e.memset(ap, val), (), writes)


class Ctx:
    pass


def rms_rstd(sch, c, xt_ap, b_x, junk_ap, b_junk, sc_ap, b_sc):
    op_stt(sch, junk_ap, xt_ap, 1.0, xt_ap, ALU.mult, ALU.mult, [b_x], [b_junk, b_sc], accum_out=sc_ap[:, 0:1])
    op_act(sch, sc_ap[:, 1:2], sc_ap[:, 0:1], AF.Sqrt, [b_sc], [b_sc], scale=1.0 / D, bias=c.eps_ap)
    sch.op("dve", lambda e: e.reciprocal(sc_ap[:, 2:3], sc_ap[:, 1:2]), [b_sc], [b_sc])


def load_bcast_vec(sch, dst_ap, b_dst, vec_ap):
    op_dma(sch, "sp", dst_ap, vec_ap.partition_broadcast(128), [], [b_dst])


def phase1(nc, sch, c, l, xsrc):
    with ExitStack() as es:
        def sb(name, shape, dt):
            return es.enter_context(nc.sbuf_tensor(f"p1_{l}_{name}", shape, dt))

        Wx = sb("Wx", [128, 8, NX], BF16)
        W_EDGES = (0, 512, 1024, 1536, 2048, 4096, N_IN, NX)
        b_Wr = sch.bufs_n(len(W_EDGES) - 1)
        R_PRIME = len(W_EDGES) - 2

        def wb(col, n):
            return [b_Wr[r] for r in range(len(W_EDGES) - 1) if col < W_EDGES[r + 1] and col + n > W_EDGES[r]]
        gb = sb("gb", [128, D], F32)
        b_gb = sch.buf()
        xt = [sb(f"xt{i}", [128, D], F32) for i in range(2)]
        b_xt = sch.bufs_n(2)
        junk = sb("junk", [128, D], BF16)
        b_junk = sch.buf()
        scr = [sb(f"scr{i}", [128, 4], F32) for i in range(2)]
        b_scr = sch.bufs_n(2)
        hb = [sb(f"hb{i}", [128, D], BF16) for i in range(2)]
        b_hb = sch.bufs_n(2)
        hT = [sb(f"hT{i}", [128, 8, 512], BF16) for i in range(2)]
        b_hT = sch.bufs_n(2)
        tab = [[sb(f"tab{i}_{j}", [128, 512], F32) for j in range(4)] for i in range(2)]
        b_tab = [sch.bufs_n(4) for _ in range(2)]
        fmo = [sb(f"fmo{i}", [128, 512], BF16) for i in range(3)]
        b_fmo = sch.bufs_n(3)
        fmf = [sb(f"fmf{i}", [128, 512], F32) for i in range(2)]
        b_fmf = sch.bufs_n(2)
        tmp = [sb(f"tmp{i}", [128, 512], F32) for i in range(4)]
        b_tmp = sch.bufs_n(4)
        tvo = [sb(f"tvo{i}", [128, 512], BF16) for i in range(2)]
        b_tvo = sch.bufs_n(2)
        gto = [sb(f"gto{i}", [128, 2048], BF16) for i in range(2)]
        b_gto = sch.bufs_n(2)
        kw = [sb(f"kw{i}", [64, 512], F32) for i in range(2)]
        b_kw = sch.bufs_n(2)
        selw = sb("selw", [40, 2, 128], F32)
        b_selw = sch.buf()
        tq = [sb(f"tq{i}", [128, 512], F32) for i in range(2)]
        h32 = [sb(f"h32{i}", [128, 512], F32) for i in range(2)]
        hib = [sb(f"hib{i}", [128, 512], BF16) for i in range(2)]
        lob = [sb(f"lob{i}", [128, 512], BF16) for i in range(2)]
        b_tq, b_h32, b_hib, b_lob = sch.bufs_n(2), sch.bufs_n(2), sch.bufs_n(2), sch.bufs_n(2)
        op_dma(sch, "sp", selw[:], c.k_selw, [], [b_selw])
        rnd = [0]

        def split_store(src_ap, b_src, npart, stores):
            r = rnd[0] % 2
            rnd[0] += 1
            op_copy(sch, "act", hib[r][0:npart, :], src_ap, [b_src], [b_hib[r]])
            op_copy(sch, "pool", h32[r][0:npart, :], hib[r][0:npart, :], [b_hib[r]], [b_h32[r]])
            op_tt(sch, "dve", lob[r][0:npart, :], src_ap, h32[r][0:npart, :], ALU.subtract, [b_src, b_h32[r]],
                  [b_lob[r]])
            for (d_hi, d_lo, r0, r1) in stores:
                op_dma(sch, "sp", d_hi, hib[r][r0:r1, :], [b_hib[r]], [])
                op_dma(sch, "sp", d_lo, lob[r][r0:r1, :], [b_lob[r]], [])

        wsrc = c.w_in[l].rearrange("(k p) n -> p k n", p=128)
        for r_ in range(R_PRIME):
            a, b = W_EDGES[r_], W_EDGES[r_ + 1]
            if b - a <= 512:
                op_dma(sch, "pool", Wx[:, :, a:b], wsrc[:, :, a:b], [], [b_Wr[r_]])
            else:
                for kc in range(8):
                    op_dma(sch, "pool", Wx[:, kc, a:b], wsrc[:, kc, a:b], [], [b_Wr[r_]])
        load_bcast_vec(sch, gb[:], b_gb, c.g_mix[l])
        for (src0, dst0, nh, hd, half) in ((C_QB, X_QB, 8, 64, 8), (C_KB, X_KB, 2, 64, 8), (C_QI, X_QI, 8, 32, 4),
                                           (C_KI, X_KI, 1, 32, 4)):
            w = nh * hd
            op_memset(sch, "dve", Wx[:, :, dst0:dst0 + w], 0.0, [b_Wr[R_PRIME]])
            for kc in range(8):
                sv = Wx[:, kc, src0:src0 + w].rearrange("p (h d) -> p h d", h=nh)
                dv = Wx[:, kc, dst0:dst0 + w].rearrange("p (h d) -> p h d", h=nh)
                op_ts(sch, "dve", dv[:, :, 0:half], sv[:, :, half:2 * half], -1.0, None, ALU.mult, None,
                      wb(src0, w), [b_Wr[R_PRIME]])
                op_copy(sch, "dve", dv[:, :, half:2 * half], sv[:, :, 0:half], wb(src0, w), [b_Wr[R_PRIME]])
        op_ts(sch, "dve", Wx[:, :, C_WI:C_WI + 8], Wx[:, :, C_WI:C_WI + 8], 1.0 / 16.0, None, ALU.mult, None,
              wb(C_WI, 8), wb(C_WI, 8))

        PB = c.PB
        ps = c.ps
        fm_cnt = [0]
        tm_cnt = [0]
        tile_cnt = [0]

        def fm_chunk(g, hTg, b_hTg, wcol, m, pcol, pm, tabk, dst, fp32out, tabs, b_tabs, obuf=None):
            k = fm_cnt[0]
            fm_cnt[0] += 1
            bank_m = 1 + (k % 2)
            bank_p = 3 + (k % 2)
            pm_ap = ps[0:m, bank_m * 512:(bank_m + 1) * 512]
            for kc in range(8):
                op_mm(sch, pm_ap, Wx[:, kc, wcol:wcol + m], hTg[:, kc, :], kc == 0, kc == 7, wb(wcol, m) + [b_hTg],
                      [PB[bank_m]])
            tsl = slice(g * 512, (g + 1) * 512)
            if pcol is None:
                o = fmo[k % 3]
                bo = b_fmo[k % 3]
                op_copy(sch, "act", o[0:m, :], pm_ap, [PB[bank_m]], [bo])
                for (d_ap, r0, r1) in dst:
                    op_dma(sch, "sp", d_ap[:, tsl], o[r0:r1, :], [bo], [])
                return
            pp_ap = ps[0:pm, bank_p * 512:(bank_p + 1) * 512]
            for kc in range(8):
                op_mm(sch, pp_ap, Wx[:, kc, pcol:pcol + pm], hTg[:, kc, :], kc == 0, kc == 7,
                      wb(pcol, pm) + [b_hTg], [PB[bank_p]])
            t1 = tmp[(2 * k) % 4]
            bt1 = b_tmp[(2 * k) % 4]
            t2 = tmp[(2 * k + 1) % 4]
            bt2 = b_tmp[(2 * k + 1) % 4]
            Ct, St = tabs[2 * tabk], tabs[2 * tabk + 1]
            bC, bS = b_tabs[2 * tabk], b_tabs[2 * tabk + 1]
            op_tt(sch, "dve", t1[0:pm, :], pm_ap[0:pm, :], Ct[0:pm, :], ALU.mult, [PB[bank_m], bC], [bt1])
            op_tt(sch, "dve", t2[0:pm, :], pp_ap, St[0:pm, :], ALU.mult, [PB[bank_p], bS], [bt2])
            if obuf is not None:
                o, bo = obuf
            elif fp32out:
                o = fmf[k % 2]
                bo = b_fmf[k % 2]
            else:
                o = fmo[k % 3]
                bo = b_fmo[k % 3]
            op_tt(sch, "pool", o[0:pm, :], t1[0:pm, :], t2[0:pm, :], ALU.add, [bt1, bt2], [bo])
            if m > pm:
                op_copy(sch, "act", o[pm:m, :], pm_ap[pm:m, :], [PB[bank_m]], [bo])
            for (d_ap, r0, r1) in dst:
                op_dma(sch, "sp", d_ap[:, tsl], o[r0:r1, :], [bo], [])
            return o, bo, k

        for g in range(8):
            hTg = hT[g % 2]
            b_hTg = b_hT[g % 2]
            tabs = tab[g % 2]
            b_tabs = b_tab[g % 2]
            tsl = slice(g * 512, (g + 1) * 512)
            for j, tsrc in enumerate((c.c_C64, c.c_S64, c.c_C32, c.c_S32)):
                op_dma(sch, "sp", tabs[j][:], tsrc[:, tsl], [], [b_tabs[j]])
            for j in range(4):
                t = g * 4 + j
                k = tile_cnt[0]
                tile_cnt[0] += 1
                x_t = xt[k % 2]
                bx = b_xt[k % 2]
                sc = scr[k % 2]
                bsc = b_scr[k % 2]
                h_t = hb[k % 2]
                bh = b_hb[k % 2]
                op_dma(sch, "sp", x_t[:], xsrc[t * 128:(t + 1) * 128, :], [], [bx])
                rms_rstd(sch, c, x_t[:], bx, junk[:], b_junk, sc[:], bsc)
                op_stt(sch, h_t[:], x_t[:], sc[:, 2:3], gb[:], ALU.mult, ALU.mult, [bx, bsc, b_gb], [bh])
                tp = ps[:, 0:512].bitcast(BF16)
                for kc in range(8):
                    op_tr(sch, tp[:, kc * 128:(kc + 1) * 128], h_t[:, kc * 128:(kc + 1) * 128], c.ident[:],
                          [bh, c.b_const], [PB[0]])
                op_copy(sch, "act", hTg[:, :, j * 128:(j + 1) * 128], tp.rearrange("p (k t) -> p k t", k=8),
                        [PB[0]], [b_hTg])
            for cc in range(4):
                fm_chunk(g, hTg, b_hTg, C_QA + 128 * cc, 128, None, 0, 0,
                         [(c.qaT[128 * cc:128 * cc + 128], 0, 128)], False, tabs, b_tabs)
            for cc in range(4):
                fm_chunk(g, hTg, b_hTg, C_KA + 128 * cc, 128, None, 0, 0,
                         [(c.kaT[128 * cc:128 * cc + 128], 0, 128)], False, tabs, b_tabs)
            for cc in range(4):
                fm_chunk(g, hTg, b_hTg, C_QB + 128 * cc, 128, X_QB + 128 * cc, 128, 0,
                         [(c.qbT[128 * cc:128 * cc + 128], 0, 128)], False, tabs, b_tabs)
            fm_chunk(g, hTg, b_hTg, C_KB, 128, X_KB, 128, 0, [(c.kbT, 0, 128)], False, tabs, b_tabs)
            kwg, b_kwg = kw[g % 2], b_kw[g % 2]
            fm_chunk(g, hTg, b_hTg, C_KI, 40, X_KI, 32, 1, [], True, tabs, b_tabs, obuf=(kwg, b_kwg))
            split_store(kwg[0:32, :], b_kwg, 32, [(c.kiS[0:32, tsl], c.kiS[32:64, tsl], 0, 32)])
            for cc in range(2):
                o, bo, kk = fm_chunk(g, hTg, b_hTg, C_QI + 128 * cc, 128, X_QI + 128 * cc, 128, 1, [], True, tabs,
                                     b_tabs)
                kx = fm_cnt[0]
                fm_cnt[0] += 1
                bank_w = 1 + (kx % 2)
                wrep = ps[:, bank_w * 512:(bank_w + 1) * 512]
                op_mm(sch, wrep, selw[:, cc, :], kwg[0:40, :], True, True, [b_selw, b_kwg], [PB[bank_w]])
                for sg, alu in enumerate((ALU.max, ALU.min)):
                    r = rnd[0] % 2
                    op_stt(sch, tq[r][:, :], wrep, 0.0, o[:, :], alu, ALU.mult, [PB[bank_w], bo], [b_tq[r]])
                    stores = []
                    for hh in range(4):
                        a = 2 * (4 * cc + hh) + sg
                        stores.append((c.qsS[a, 0:32, tsl], c.qsS[a, 32:64, tsl], hh * 32, (hh + 1) * 32))
                    split_store(tq[r][:, :], b_tq[r], 128, stores)
            for j in range(4):
                t = g * 4 + j
                rows = slice(t * 128, (t + 1) * 128)
                lhs = [hTg[:, kc, j * 128:(j + 1) * 128] for kc in range(8)]
                gt = gto[t % 2]
                bg = b_gto[t % 2]
                for (wcol, n, kind) in ((C_VA, 512, "va"), (C_VB, 128, "vb"), (C_G, 512, 0), (C_G + 512, 512, 1),
                                        (C_G + 1024, 512, 2), (C_G + 1536, 512, 3)):
                    k = tm_cnt[0]
                    tm_cnt[0] += 1
                    bank = 5 + (k % 3)
                    p_ap = ps[:, bank * 512:bank * 512 + n]
                    for kc in range(8):
                        op_mm(sch, p_ap, lhs[kc], Wx[:, kc, wcol:wcol + n], kc == 0, kc == 7,
                              wb(wcol, n) + [b_hTg], [PB[bank]])
                    if kind == "va" or kind == "vb":
                        o = tvo[k % 2]
                        bo = b_tvo[k % 2]
                        op_copy(sch, "act", o[:, 0:n], p_ap, [PB[bank]], [bo])
                        dstt = c.va if kind == "va" else c.vb
                        op_dma(sch, "sp", dstt[rows, :], o[:, 0:n], [bo], [])
                    else:
                        op_act(sch, gt[:, kind * 512:(kind + 1) * 512], p_ap, AF.Sigmoid, [PB[bank]], [bg])
                op_dma(sch, "sp", c.gates[rows, :], gt[:], [bg], [])
        sch.flush()


def phase2(nc, sch, c, l):
    CW = 1024
    with ExitStack() as es:
        def sb(name, shape, dt):
            return es.enter_context(nc.sbuf_tensor(f"p2_{l}_{name}", shape, dt))

        QT = [sb(f"QT{i}", [64, S], BF16) for i in range(2)]
        KT = [sb(f"KT{i}", [64, S], BF16) for i in range(2)]
        b_QK = sch.bufs_n(2)
        V = sb("V", [128, NT, 512], BF16)
        b_V = sch.buf()
        om = [sb(f"om{i}", [128, S], F32) for i in range(2)]
        b_om = [sch.bufs_n(S // CW) for _ in range(2)]
        P = [sb(f"P{i}", [128, S + 4], F32) for i in range(2)]
        b_P = [sch.bufs_n(S // CW + 1) for _ in range(2)]
        NA = 6
        A = [sb(f"A{i}", [128, CW], BF16) for i in range(NA)]
        b_A = sch.bufs_n(NA)
        AT = [sb(f"AT{i}", [128, CW], BF16) for i in range(3)]
        b_AT = sch.bufs_n(3)
        ya = sb("ya", [128, NT, 512], BF16)
        b_ya = sch.buf()
        PB = c.PB
        ps = c.ps

        vsrc = c.va.rearrange("(n p) d -> p n d", p=128)
        for q in range(4):
            op_dma(sch, "sp", V[:, q * 8:(q + 1) * 8, :], vsrc[:, q * 8:(q + 1) * 8, :], [], [b_V])

        def load_head(h):
            s = h % 2
            op_dma(sch, "sp", QT[s][:], c.qaT[h * 64:(h + 1) * 64, :], [], [b_QK[s]])
            op_dma(sch, "sp", KT[s][:], c.kaT[h * 64:(h + 1) * 64, :], [], [b_QK[s]])

        jobs = []
        for h in range(8):
            for i in range(NT):
                n = (i + 1) * 128
                nch = (n + CW - 1) // CW
                for cc in reversed(range(nch)):
                    jobs.append((h, i, cc, nch))
        LAG = 3
        zc = [0]
        tc_ = [0]

        def stage1(j):
            h, i, cc, nch = jobs[j]
            n = (i + 1) * 128
            c0 = cc * CW
            ln = min(CW, n - c0)
            s = h % 2
            slot = (h * NT + i) % 2
            first = (cc == nch - 1)
            if first and i == 0:
                if h + 1 < 8:
                    load_head(h + 1)
            if first:
                op_memset(sch, "dve", P[slot][:, n:n + 1], 1.0, [b_P[slot][n // CW]])
            zb = 2 * (zc[0] % 2)
            zc[0] += 1
            zw = [PB[zb], PB[zb + 1]]
            q_ap = QT[s][:, i * 128:(i + 1) * 128]
            for sub in range(0, ln, 512):
                sln = min(512, ln - sub)
                lastsub = (sub + sln == ln)
                op_mm(sch, ps[:, zb * 512 + sub:zb * 512 + sub + sln], q_ap, KT[s][:, c0 + sub:c0 + sub + sln], True,
                      not (first and lastsub), [b_QK[s]], zw)
            if first:
                op_mm(sch, ps[:, zb * 512 + ln - 128:zb * 512 + ln], c.ident[:], c.maskA[:], False, True,
                      [c.b_const], zw)
            op_act(sch, om[slot][:, c0:c0 + ln], ps[:, zb * 512:zb * 512 + ln], AF.Sigmoid, zw, [b_om[slot][cc]],
                   scale=-0.125)
            nxt = (c0 + ln) // CW
            sch.op("dve", lambda e: e.tensor_tensor_scan(out=P[slot][:, c0:c0 + ln][:, ::-1],
                                                         data0=om[slot][:, c0:c0 + ln][:, ::-1],
                                                         data1=om[slot][:, c0:c0 + ln][:, ::-1],
                                                         initial=P[slot][:, c0 + ln:c0 + ln + 1],
                                                         op0=ALU.mult, op1=ALU.bypass),
                   [b_om[slot][cc], b_P[slot][nxt]], [b_P[slot][cc]])
            a = A[j % NA]
            op_tt(sch, "pool", a[:, 0:ln], P[slot][:, c0 + 1:c0 + ln + 1], P[slot][:, c0:c0 + ln], ALU.subtract,
                  [b_P[slot][cc], b_P[slot][nxt]], [b_A[j % NA]])

        def stage2(j):
            h, i, cc, nch = jobs[j]
            n = (i + 1) * 128
            c0 = cc * CW
            ln = min(CW, n - c0)
            nb = ln // 128
            a = A[j % NA]
            k = tc_[0]
            tc_[0] += 1
            tbank = (4, 7)[k % 2]
            tp = ps[:, tbank * 512:tbank * 512 + 512].bitcast(BF16)
            for b in range(nb):
                op_tr(sch, tp[:, b * 128:(b + 1) * 128], a[:, b * 128:(b + 1) * 128], c.ident[:],
                      [b_A[j % NA], c.b_const], [PB[tbank]])
            at = AT[k % 3]
            op_copy(sch, "act", at[:, 0:ln], tp[:, 0:ln], [PB[tbank]], [b_AT[k % 3]])
            ybank = 5 + ((h * NT + i) % 2)
            y = ps[:, ybank * 512:ybank * 512 + 64]
            for b in range(nb):
                kb = (c0 // 128) + b
                first = (cc == nch - 1) and b == 0
                last = (cc == 0) and b == nb - 1
                op_mm(sch, y, at[:, b * 128:(b + 1) * 128], V[:, kb, h * 64:(h + 1) * 64], first, last,
                      [b_AT[k % 3], b_V], [PB[ybank]])
            if cc == 0:
                op_copy(sch, "act", ya[:, i, h * 64:(h + 1) * 64], y, [PB[ybank]], [b_ya])

        load_head(0)
        J = len(jobs)
        for j in range(J + LAG):
            if j < J:
                stage1(j)
            if j - LAG >= 0:
                stage2(j - LAG)
        ydst = c.ya.rearrange("(n p) d -> p n d", p=128)
        for q in range(4):
            op_dma(sch, "sp", ydst[:, q * 8:(q + 1) * 8, :], ya[:, q * 8:(q + 1) * 8, :], [b_ya], [])
        sch.flush()


def phase3(nc, sch, c, l):
    with ExitStack() as es:
        def sb(name, shape, dt):
            return es.enter_context(nc.sbuf_tensor(f"p3_{l}_{name}", shape, dt))

        kiR = sb("kiR", [96, S], BF16)
        b_ki = sch.buf()
        qst = [sb(f"qst{i}", [96, 16, 128], BF16) for i in range(2)]
        b_qst = sch.bufs_n(2)
        d_qsS = sch.buf()
        d_kiS = sch.buf()
        qbT = sb("qbT", [128, 4, S], BF16)
        b_qb = sch.buf()
        kbR = sb("kbR", [128, 2, S], BF16)
        b_kb = sch.buf()
        Vb = sb("Vb", [128, NT, 2, 65], BF16)
        b_V = sch.buf()
        ybu = [sb(f"ybu{i}", [128, 8, 65], F32) for i in range(2)]
        b_ybu = sch.bufs_n(2)
        r8 = [sb(f"r8{i}", [128, 8], F32) for i in range(2)]
        m8 = [sb(f"m8{i}", [128, 32, 8], F32) for i in range(2)]
        acc = [sb(f"acc{i}", [128, S], F32) for i in range(2)]
        b_acc = [sch.bufs_n(8) for _ in range(2)]
        mb = [sb(f"mb{i}", [128, S], BF16) for i in range(2)]
        b_mb = sch.bufs_n(2)
        jk = sb("jk", [128, S], BF16)
        b_jk = sch.buf()
        bs = [sb(f"bs{i}", [128, 8], F32) for i in range(2)]
        b_bs = sch.bufs_n(2)
        steps = [sb(f"steps{i}", [128, KBIS], F32) for i in range(2)]
        NE = 7
        E = [sb(f"E{i}", [128, 512], BF16) for i in range(3)]
        b_E = sch.bufs_n(3)
        EM = [sb(f"EM{i}", [128, 512], BF16) for i in range(NE)]
        b_EM = sch.bufs_n(NE)
        MT = [sb(f"MT{i}", [128, S], BF16) for i in range(2)]
        b_MT = sch.bufs_n(2)
        rsp = [sb(f"rsp{i}", [128, 12], F32) for i in range(4)]
        b_rsp = sch.bufs_n(4)
        ybt = [sb(f"ybt{i}", [128, 512], BF16) for i in range(2)]
        b_ybt = sch.bufs_n(2)
        PB = c.PB
        ps = c.ps

        op_dma(sch, "sp", kiR[0:32, :], c.kiS[0:32, :], [], [b_ki])
        op_dma(sch, "sp", kiR[32:64, :], c.kiS[0:32, :], [], [b_ki])
        op_dma(sch, "sp", kiR[64:96, :], c.kiS[32:64, :], [], [b_ki])
        qsrc = c.qbT.rearrange("(k p) t -> p k t", p=128)
        for k in range(4):
            op_dma(sch, "sp", qbT[:, k, :], qsrc[:, k, :], [], [b_qb])
        for g in range(2):
            for r in range(2):
                op_dma(sch, "sp", kbR[r * 64:(r + 1) * 64, g, :], c.kbT[g * 64:(g + 1) * 64, :], [], [b_kb])
        vsrc = c.vb.rearrange("(n p) d -> p n d", p=128)
        op_memset(sch, "pool", Vb[:, :, :, 64:65], 1.0, [b_V])
        for q in range(4):
            for g in range(2):
                op_dma(sch, "sp", Vb[:, q * 8:(q + 1) * 8, g, 0:64], vsrc[:, q * 8:(q + 1) * 8, g * 64:(g + 1) * 64],
                       [], [b_V])
        sc_cnt = [0]

        def score_units(i):
            units = []
            n = (i + 1) * 128
            nch2 = (n + 1023) // 1024
            sl = i % 2

            def load():
                op_dma(sch, "sp", qst[sl][0:64, :, :],
                       c.qsS[:, :, i * 128:(i + 1) * 128].rearrange("a r t -> r a t"), [d_qsS], [b_qst[sl]])
                op_dma(sch, "sp", qst[sl][64:96, :, :],
                       c.qsS[:, 0:32, i * 128:(i + 1) * 128].rearrange("a r t -> r a t"), [d_qsS], [b_qst[sl]])

            def pair(cc, a_idx, first):
                c0 = cc * 1024
                ln = min(1024, n - c0)
                a_ap = acc[sl][:, c0:c0 + ln]
                ab = b_acc[sl][2 * cc:2 * cc + (2 if ln > 512 else 1)]
                diag = (cc == nch2 - 1)
                alu = ALU.max if (a_idx % 2 == 0) else ALU.min
                k = sc_cnt[0]
                sc_cnt[0] += 1
                zb = (2, 6)[k % 2]
                zw = [PB[zb], PB[zb + 1]]
                for sub in range(0, ln, 512):
                    sln = min(512, ln - sub)
                    op_mm(sch, ps[:, zb * 512 + sub:zb * 512 + sub + sln], qst[sl][:, a_idx, :],
                          kiR[:, c0 + sub:c0 + sub + sln], True, True, [b_qst[sl], b_ki], zw)
                y = ps[:, zb * 512:zb * 512 + ln]
                if first:
                    in1 = c.initdiag[:, 1024 - ln:1024] if diag else c.zeros[:, 0:ln]
                    op_stt(sch, a_ap, y, 0.0, in1, alu, ALU.add, zw + [c.b_const], ab)
                else:
                    op_stt(sch, a_ap, y, 0.0, a_ap, alu, ALU.add, zw + ab, ab)

            for cc in range(nch2):
                for a_idx in range(16):
                    if cc == 0 and a_idx == 0:
                        units.append(lambda: (load(), pair(0, 0, True)))
                    else:
                        units.append((lambda cc, a_idx: (lambda: pair(cc, a_idx, a_idx == 0)))(cc, a_idx))
            return units

        mt_cnt = [0]

        def mask_T(i):
            n = (i + 1) * 128
            sl = i % 2
            for c0 in range(0, n, 512):
                ln = min(512, n - c0)
                k = mt_cnt[0]
                mt_cnt[0] += 1
                tbank = 2
                tp = ps[:, tbank * 512:tbank * 512 + 512].bitcast(BF16)
                for b in range(ln // 128):
                    op_tr(sch, tp[:, b * 128:(b + 1) * 128], mb[sl][:, c0 + b * 128:c0 + (b + 1) * 128], c.ident[:],
                          [b_mb[sl], c.b_const], [PB[tbank]])
                op_copy(sch, "act", MT[sl][:, c0:c0 + ln], tp[:, 0:ln], [PB[tbank]], [b_MT[sl]])

        def thresh_steps(i):
            n = (i + 1) * 128
            nch = (n + 511) // 512
            sl = i % 2
            m = mb[sl]
            if i < 2:
                def const_mask():
                    if n > 128:
                        op_memset(sch, "pool", m[:, 0:n - 128], 1.0, [b_mb[sl]])
                    op_copy(sch, "pool", m[:, n - 128:n], c.mbdiag[:], [c.b_const], [b_mb[sl]])
                    mask_T(i)
                return [const_mask]
            a = acc[sl]
            ba = b_acc[sl][0:nch]
            b = bs[sl]
            bb = b_bs[sl]
            st = steps[sl]
            mm8 = m8[sl]
            seg = (n - 128) // 32

            def init():
                for jj in range(32):
                    sch.op("dve", (lambda jj: (lambda e: e.max(out=mm8[:, jj, :],
                                                               in_=a[:, jj * seg:(jj + 1) * seg])))(jj), ba, [bb])
                sch.op("dve", lambda e: e.tensor_reduce(out=b[:, 0:1], in_=mm8[:, :, 7], axis=AX.X, op=ALU.min), [bb],
                       [bb])
                sch.op("dve", lambda e: e.tensor_reduce(out=b[:, 6:7], in_=mm8[:, :, 7], axis=AX.X, op=ALU.max), [bb],
                       [bb])
                sch.op("dve", lambda e: e.tensor_reduce(out=b[:, 7:8], in_=a[:, n - 128:n], axis=AX.X, op=ALU.max),
                       ba, [bb])
                op_tt(sch, "dve", b[:, 1:2], b[:, 6:7], b[:, 7:8], ALU.max, [bb], [bb])
                op_tt(sch, "dve", b[:, 2:3], b[:, 1:2], b[:, 0:1], ALU.subtract, [bb], [bb])
                op_ts(sch, "dve", st[:, :], c.pow2[:, :], b[:, 2:3], None, ALU.mult, None, [bb, c.b_const], [bb])
                op_ts(sch, "dve", b[:, 3:4], b[:, 0:1], st[:, 0:1], -1.0, ALU.add, ALU.mult, [bb], [bb])

            def it_a(k):
                op_act(sch, jk[:, 0:n], a[:, 0:n], AF.Sign, ba + [bb], [b_jk, bb], bias=b[:, 3:4], scale=1.0,
                       accum_out=b[:, 4:5])

            def it_b(k):
                op_ts(sch, "dve", b[:, 5:6], b[:, 4:5], 511.0 - n, st[:, k:k + 1], ALU.is_ge, ALU.mult, [bb], [bb])
                op_tt(sch, "dve", b[:, 0:1], b[:, 0:1], b[:, 5:6], ALU.add, [bb], [bb])
                if k + 1 < KBIS:
                    op_ts(sch, "dve", b[:, 3:4], b[:, 0:1], st[:, k + 1:k + 2], -1.0, ALU.add, ALU.mult, [bb], [bb])

            def fin():
                op_ts(sch, "dve", m[:, 0:n], a[:, 0:n], b[:, 0:1], None, ALU.is_ge, None, ba + [bb], [b_mb[sl]])
                mask_T(i)

            steps_ = [init]
            for k in range(KBIS):
                steps_.append((lambda k: (lambda: it_a(k)))(k))
                steps_.append((lambda k: (lambda: it_b(k)))(k))
            return steps_ + [fin]

        jobs = []
        order = list(range(NT - 1, -1, -1))
        for i in order:
            n = (i + 1) * 128
            nch = (n + 511) // 512
            for h in range(8):
                for cc in range(nch):
                    jobs.append((i, h, cc, nch))
        LAG = 5
        zc = [0]
        tc_ = [0]

        def stage1(j):
            i, h, cc, nch = jobs[j]
            n = (i + 1) * 128
            c0 = cc * 512
            ln = min(512, n - c0)
            nb = ln // 128
            sl = i % 2
            g = h // 4
            pb = (h % 2) * 64
            bank = zc[0] % 2
            zc[0] += 1
            q_ap = qbT[pb:pb + 64, h // 2, i * 128:(i + 1) * 128]
            for b in range(nb):
                op_mm(sch, ps[:, bank * 512 + b * 128:bank * 512 + (b + 1) * 128],
                      kbR[pb:pb + 64, g, c0 + b * 128:c0 + (b + 1) * 128], q_ap, True, True, [b_qb, b_kb],
                      [PB[bank]])
            e_ = E[j % 3]
            op_act(sch, e_[:, 0:ln], ps[:, bank * 512:bank * 512 + ln], AF.Exp, [PB[bank]], [b_E[j % 3]], scale=0.125)
            op_tt(sch, "pool", EM[j % NE][:, 0:ln], e_[:, 0:ln], MT[sl][:, c0:c0 + ln], ALU.mult,
                  [b_E[j % 3], b_MT[sl]], [b_EM[j % NE]])

        def stage2(j):
            i, h, cc, nch = jobs[j]
            n = (i + 1) * 128
            c0 = cc * 512
            ln = min(512, n - c0)
            nb = ln // 128
            g = h // 4
            em = EM[j % NE]
            ybank = 4 + ((i * 8 + h) % 2)
            y = ps[:, ybank * 512:ybank * 512 + 65]
            for b in range(nb):
                kb = (c0 // 128) + b
                op_mm(sch, y, em[:, b * 128:(b + 1) * 128], Vb[:, kb, g, :], cc == 0 and b == 0,
                      cc == nch - 1 and b == nb - 1, [b_EM[j % NE], b_V], [PB[ybank]])
            if cc == nch - 1:
                yu = ybu[i % 2]
                byu = b_ybu[i % 2]
                op_copy(sch, "act", yu[:, h, :], y, [PB[ybank]], [byu])
                if h == 7:
                    rr8 = r8[i % 2]
                    yt = ybt[i % 2]
                    sch.op("dve", lambda e: e.reciprocal(rr8[:, :], yu[:, :, 64]), [byu], [byu])
                    for hh in range(8):
                        op_ts(sch, "dve", yt[:, hh * 64:(hh + 1) * 64], yu[:, hh, 0:64], rr8[:, hh:hh + 1], None,
                              ALU.mult, None, [byu], [b_ybt[i % 2]])
                    op_dma(sch, "sp", c.yb[i * 128:(i + 1) * 128, :], yt[:], [b_ybt[i % 2]], [])

        starts = {}
        ends = {}
        for j, (i, h, cc, nch) in enumerate(jobs):
            starts.setdefault(i, j)
            ends[i] = j + 1
        J = len(jobs)

        t0, t1 = order[0], order[1]
        for u in score_units(t0):
            u()
        thr0 = thresh_steps(t0)
        scu1 = score_units(t1)
        si = 0
        for ti, u in enumerate(thr0):
            u()
            s_end = min(len(scu1), ((ti + 1) * len(scu1)) // len(thr0))
            while si < s_end:
                scu1[si]()
                si += 1
        while si < len(scu1):
            scu1[si]()
            si += 1
        for p, i in enumerate(order):
            thr = thresh_steps(order[p + 1]) if p + 1 < NT else []
            scu = []
            if p + 2 < NT and order[p + 2] >= 2:
                scu = score_units(order[p + 2])
            nj = ends[i] - starts[i]
            ti = si = 0
            for jj, j in enumerate(range(starts[i], ends[i])):
                stage1(j)
                if j - LAG >= 0:
                    stage2(j - LAG)
                t_end = min(len(thr), ((jj + 1) * len(thr)) // max(1, int(0.8 * nj)))
                s_end = min(len(scu), ((jj + 1) * len(scu)) // max(1, int(0.9 * nj)))
                while si < s_end or ti < t_end:
                    if si < s_end:
                        scu[si]()
                        si += 1
                    if ti < t_end:
                        thr[ti]()
                        ti += 1
        for j in range(J - LAG, J):
            stage2(j)
        sch.flush()


def phase4a(nc, sch, c, l, xsrc, W2pre=None):
    with ExitStack() as es:
        def sb(name, shape, dt):
            return es.enter_context(nc.sbuf_tensor(f"p4a_{l}_{name}", shape, dt))

        Wua = sb("Wua", [128, 4, D], BF16)
        Wub = sb("Wub", [128, 4, D], BF16)
        Wo = sb("Wo", [128, 8, D], BF16)
        b_Wu = sch.buf()
        b_Wo = sch.buf()
        yt = [sb(f"yt{i}", [128, 1024], BF16) for i in range(3)]
        b_yt = sch.bufs_n(3)
        gt = [sb(f"gt{i}", [128, 2048], BF16) for i in range(3)]
        b_gt = sch.bufs_n(3)
        xt = [sb(f"xt{i}", [128, D], F32) for i in range(3)]
        b_xt = sch.bufs_n(3)
        yT = [sb(f"yT{i}", [128, 8, 128], BF16) for i in range(2)]
        b_yT = sch.bufs_n(2)
        m1 = [sb(f"m1_{i}", [128, D], F32) for i in range(2)]
        m2 = [sb(f"m2_{i}", [128, D], F32) for i in range(2)]
        b_m1 = sch.bufs_n(2)
        b_m2 = sch.bufs_n(2)
        mg = [sb(f"mg{i}", [128, D], BF16) for i in range(2)]
        b_mg = sch.bufs_n(2)
        mT = [sb(f"mT{i}", [128, 8, 128], BF16) for i in range(2)]
        b_mT = sch.bufs_n(2)
        PB = c.PB
        ps = c.ps
        for (dstw, srcw, nk) in ((Wua, c.w_up_a[l], 4), (Wub, c.w_up_b[l], 4), (Wo, c.w_o[l], 8)):
            sv = srcw.rearrange("(k p) n -> p k n", p=128)
            for kc in range(nk):
                op_dma(sch, "pool", dstw[:, kc, :], sv[:, kc, :], [], [b_Wo if nk == 8 else b_Wu])
        if W2pre is not None:
            b_pre = sch.buf()
            s2 = c.w_ff2[l].rearrange("(k p) n -> p k n", p=128)
            for q in range(8):
                op_dma(sch, "pool", W2pre[:, q * 4:(q + 1) * 4, :], s2[:, q * 4:(q + 1) * 4, :], [], [b_pre])

        def loads(t):
            s3 = t % 3
            rows = slice(t * 128, (t + 1) * 128)
            op_dma(sch, "sp", yt[s3][:, 0:512], c.ya[rows, :], [], [b_yt[s3]])
            op_dma(sch, "sp", yt[s3][:, 512:1024], c.yb[rows, :], [], [b_yt[s3]])
            op_dma(sch, "sp", gt[s3][:], c.gates[rows, :], [], [b_gt[s3]])
            op_dma(sch, "sp", xt[s3][:], xsrc[rows, :], [], [b_xt[s3]])

        def stage_a(t):
            s = t % 2
            s3 = t % 3
            tp = ps[:, 0:512].bitcast(BF16)
            for kc in range(8):
                op_tr(sch, tp[:, kc * 128:(kc + 1) * 128], yt[s3][:, kc * 128:(kc + 1) * 128], c.ident[:],
                      [b_yt[s3], c.b_const], [PB[0]])
            op_copy(sch, "act", yT[s][:], tp.rearrange("p (k t) -> p k t", k=8), [PB[0]], [b_yT[s]])
            for (W_, koff, bank0) in ((Wua, 0, 1), (Wub, 4, 3)):
                for sl in range(2):
                    p_ap = ps[:, (bank0 + sl) * 512:(bank0 + sl + 1) * 512]
                    for kc in range(4):
                        op_mm(sch, p_ap, yT[s][:, koff + kc, :], W_[:, kc, sl * 512:(sl + 1) * 512], kc == 0, kc == 3,
                              [b_yT[s], b_Wu], [PB[bank0 + sl]])
            op_tt(sch, "dve", m1[s][:], ps[:, 512:1536], gt[s3][:, 0:1024], ALU.mult, [PB[1], PB[2], b_gt[s3]],
                  [b_m1[s]])
            op_tt(sch, "dve", m2[s][:], ps[:, 1536:2560], gt[s3][:, 1024:2048], ALU.mult, [PB[3], PB[4], b_gt[s3]],
                  [b_m2[s]])
            op_tt(sch, "pool", mg[s][:], m1[s][:], m2[s][:], ALU.add, [b_m1[s], b_m2[s]], [b_mg[s]])

        def stage_b(t):
            s = t % 2
            s3 = t % 3
            rows = slice(t * 128, (t + 1) * 128)
            tp2 = ps[:, 5 * 512:6 * 512].bitcast(BF16)
            for kc in range(8):
                op_tr(sch, tp2[:, kc * 128:(kc + 1) * 128], mg[s][:, kc * 128:(kc + 1) * 128], c.ident[:],
                      [b_mg[s], c.b_const], [PB[5]])
            op_copy(sch, "act", mT[s][:], tp2.rearrange("p (k t) -> p k t", k=8), [PB[5]], [b_mT[s]])
            for sl in range(2):
                p_ap = ps[:, (6 + sl) * 512:(7 + sl) * 512]
                for kc in range(8):
                    op_mm(sch, p_ap, mT[s][:, kc, :], Wo[:, kc, sl * 512:(sl + 1) * 512], kc == 0, kc == 7,
                          [b_mT[s], b_Wo], [PB[6 + sl]])
            op_tt(sch, "dve", xt[s3][:], xt[s3][:], ps[:, 6 * 512:8 * 512], ALU.add, [b_xt[s3], PB[6], PB[7]],
                  [b_xt[s3]])
            op_dma(sch, "sp", c.xs[rows, :], xt[s3][:], [b_xt[s3]], [])

        loads(0)
        for t in range(NT + 1):
            if t + 1 < NT:
                loads(t + 1)
            if t < NT:
                stage_a(t)
            if t >= 1:
                stage_b(t - 1)
        sch.flush()


def phase4b(nc, sch, c, l, last, W2pre=None):
    G = 256
    with ExitStack() as es:
        def sb(name, shape, dt):
            return es.enter_context(nc.sbuf_tensor(f"p4b_{l}_{name}", shape, dt))

        W1 = sb("W1", [128, 8, DFF], BF16)
        W2 = W2pre if W2pre is not None else sb("W2", [128, 32, D], BF16)
        b_W1 = sch.bufs_n(8)
        b_W2 = sch.buf()
        gb = sb("gb", [128, D], F32)
        gf = sb("gf", [128, D], F32)
        b_gb = sch.buf()
        xt = [sb(f"xt{i}", [128, D], F32) for i in range(4)]
        b_xt = sch.bufs_n(4)
        junk = sb("junk", [128, D], BF16)
        b_junk = sch.buf()
        scr = [sb(f"scr{i}", [128, 4], F32) for i in range(2)]
        b_scr = sch.bufs_n(2)
        hb = [sb(f"hb{i}", [128, D], BF16) for i in range(2)]
        b_hb = sch.bufs_n(2)
        hT = [sb(f"hT{i}", [128, 8, G], BF16) for i in range(2)]
        b_hT = sch.bufs_n(2)
        rr = [sb(f"rr{i}", [128, G], F32) for i in range(2)]
        b_rr = sch.bufs_n(2)
        uT = sb("uT", [128, 32, G], BF16)
        b_uT = sch.buf()
        ot = [sb(f"ot{i}", [128, D], F32) for i in range(2)]
        b_ot = sch.bufs_n(2)
        PB = c.PB
        ps = c.ps
        s1 = c.w_ff1[l].rearrange("(k p) n -> p k n", p=128)
        for blk in range(8):
            op_dma(sch, "pool", W1[:, :, blk * 512:(blk + 1) * 512], s1[:, :, blk * 512:(blk + 1) * 512], [],
                   [b_W1[blk]])
        if W2pre is None:
            s2 = c.w_ff2[l].rearrange("(k p) n -> p k n", p=128)
            for q in range(8):
                op_dma(sch, "pool", W2[:, q * 4:(q + 1) * 4, :], s2[:, q * 4:(q + 1) * 4, :], [], [b_W2])
        load_bcast_vec(sch, gb[:], b_gb, c.g_mlp[l])
        if last:
            load_bcast_vec(sch, gf[:], b_gb, c.g_final)
        ntg = G // 128
        tcnt = [0]
        fcnt = [0]
        ocnt = [0]
        for g in range(S // G):
            hTg = hT[g % 2]
            bhT = b_hT[g % 2]
            xts = []
            for j in range(ntg):
                t = g * ntg + j
                k = tcnt[0]
                tcnt[0] += 1
                x_t = xt[k % 4]
                bx = b_xt[k % 4]
                xts.append((x_t, bx, t))
                sc = scr[k % 2]
                bsc = b_scr[k % 2]
                h_t = hb[k % 2]
                bh = b_hb[k % 2]
                op_dma(sch, "sp", x_t[:], c.xs[t * 128:(t + 1) * 128, :], [], [bx])
                rms_rstd(sch, c, x_t[:], bx, junk[:], b_junk, sc[:], bsc)
                op_stt(sch, h_t[:], x_t[:], sc[:, 2:3], gb[:], ALU.mult, ALU.mult, [bx, bsc, b_gb], [bh])
                tp = ps[:, 0:512].bitcast(BF16)
                for kc in range(8):
                    op_tr(sch, tp[:, kc * 128:(kc + 1) * 128], h_t[:, kc * 128:(kc + 1) * 128], c.ident[:],
                          [bh, c.b_const], [PB[0]])
                op_copy(sch, "act", hTg[:, :, j * 128:(j + 1) * 128], tp.rearrange("p (k t) -> p k t", k=8), [PB[0]],
                        [bhT])
            for fc in range(32):
                k = fcnt[0]
                fcnt[0] += 1
                bank = 1 + (k % 3)
                p_ap = ps[:, bank * 512:bank * 512 + G]
                for kc in range(8):
                    op_mm(sch, p_ap, W1[:, kc, fc * 128:(fc + 1) * 128], hTg[:, kc, :], kc == 0, kc == 7,
                          [b_W1[fc // 4], bhT], [PB[bank]])
                r = rr[k % 2]
                op_act(sch, r[:], p_ap, AF.Relu, [PB[bank]], [b_rr[k % 2]])
                op_tt(sch, "pool" if (k % 2) else "dve", uT[:, fc, :], r[:], r[:], ALU.mult, [b_rr[k % 2]], [b_uT])
            for j in range(ntg):
                x_t, bx, t = xts[j]
                k = ocnt[0]
                ocnt[0] += 1
                bank0 = 4 + 2 * (k % 2)
                for sl in range(2):
                    p_ap = ps[:, (bank0 + sl) * 512:(bank0 + sl + 1) * 512]
                    for fc in range(32):
                        op_mm(sch, p_ap, uT[:, fc, j * 128:(j + 1) * 128], W2[:, fc, sl * 512:(sl + 1) * 512],
                              fc == 0, fc == 31, [b_uT, b_W2], [PB[bank0 + sl]])
                op_tt(sch, "dve", x_t[:], x_t[:], ps[:, bank0 * 512:(bank0 + 2) * 512], ALU.add,
                      [bx, PB[bank0], PB[bank0 + 1]], [bx])
                rows = slice(t * 128, (t + 1) * 128)
                if not last:
                    op_dma(sch, "sp", c.xs[rows, :], x_t[:], [bx], [])
                else:
                    sc = scr[k % 2]
                    bsc = b_scr[k % 2]
                    o_t = ot[k % 2]
                    bo = b_ot[k % 2]
                    rms_rstd(sch, c, x_t[:], bx, junk[:], b_junk, sc[:], bsc)
                    op_stt(sch, o_t[:], x_t[:], sc[:, 2:3], gf[:], ALU.mult, ALU.mult, [bx, bsc, b_gb], [bo])
                    op_dma(sch, "sp", c.out[rows, :], o_t[:], [bo], [])
        sch.flush()


def build_nc(depth=DEPTH):
    nc = bass.Bass("TRN2", target_bir_lowering=False)
    c = Ctx()

    def din(name, shape):
        return nc.dram_tensor(name, shape, F32, kind="ExternalInput").ap()

    c.x = din("x", [S, D])
    c.g_mix = din("g_mix", [DEPTH, D])
    c.w_in = din("w_in", [DEPTH, D, N_IN])
    c.w_up_a = din("w_up_a", [DEPTH, 512, D])
    c.w_up_b = din("w_up_b", [DEPTH, 512, D])
    c.w_o = din("w_o", [DEPTH, D, D])
    c.g_mlp = din("g_mlp", [DEPTH, D])
    c.w_ff1 = din("w_ff1", [DEPTH, D, DFF])
    c.w_ff2 = din("w_ff2", [DEPTH, DFF, D])
    c.g_final = din("g_final", [D])
    k_ident = din("k_ident", [128, 128])
    k_bigident = din("k_bigident", [128, 128])
    k_maskA = din("k_maskA", [128, 128])
    k_mbdiag = din("k_mbdiag", [128, 128])
    k_initdiag = din("k_initdiag", [128, 1024])
    k_pow2 = din("k_pow2", [128, KBIS])
    c.k_selw = din("k_selw", [40, 2, 128])
    c.c_C64 = din("k_C64", [128, S])
    c.c_S64 = din("k_S64", [128, S])
    c.c_C32 = din("k_C32", [128, S])
    c.c_S32 = din("k_S32", [128, S])
    c.out = nc.dram_tensor("out", [S, D], F32, kind="ExternalOutput").ap()

    dump = DBG["dump"]

    def scratch(name, shape, dt):
        return nc.dram_tensor(name, shape, dt, kind=("ExternalOutput" if dump else "Internal")).ap()

    c.xs = scratch("xs", [S, D], F32)
    c.qaT = scratch("qaT", [512, S], BF16)
    c.kaT = scratch("kaT", [512, S], BF16)
    c.va = scratch("va", [S, 512], BF16)
    c.qbT = scratch("qbT", [512, S], BF16)
    c.kbT = scratch("kbT", [128, S], BF16)
    c.vb = scratch("vb", [S, 128], BF16)
    c.qiT = scratch("qiT", [256, S], F32)
    c.kiT = scratch("kiT", [32, S], F32)
    c.wT = scratch("wT", [8, S], F32)
    c.qsS = scratch("qsS", [16, 64, S], BF16)
    c.kiS = scratch("kiS", [64, S], BF16)
    c.gates = scratch("gates", [S, 2048], BF16)
    c.ya = scratch("ya", [S, 512], BF16)
    c.yb = scratch("yb", [S, 512], BF16)

    with ExitStack() as es:
        def sbg(name, shape, dt):
            return es.enter_context(nc.sbuf_tensor(name, shape, dt))

        c.ident = sbg("ident", [128, 128], BF16)
        c.bigident = sbg("bigident", [128, 128], BF16)
        c.maskA = sbg("maskA", [128, 128], BF16)
        c.mbdiag = sbg("mbdiag", [128, 128], BF16)
        c.initdiag = sbg("initdiag", [128, 1024], F32)
        c.zeros = sbg("zeros", [128, 1024], F32)
        c.pow2 = sbg("pow2", [128, KBIS], F32)
        eps_t = sbg("eps", [128, 1], F32)
        c.eps_ap = eps_t[:, 0:1]
        c.ps = es.enter_context(nc.psum_tensor("ps", [128, 4096], F32))
        block = es.enter_context(nc.Block())
        sch = Sched(nc, block)
        c.PB = sch.bufs_n(8, "psb")
        c.b_const = sch.buf("const")
        for (dst, src) in ((c.ident, k_ident), (c.bigident, k_bigident), (c.maskA, k_maskA), (c.mbdiag, k_mbdiag)):
            op_dma(sch, "pool", dst[:], src, [], [c.b_const])
        op_dma(sch, "sp", c.initdiag[:], k_initdiag, [], [c.b_const])
        op_dma(sch, "sp", c.pow2[:], k_pow2, [], [c.b_const])
        op_memset(sch, "dve", c.zeros[:], 0.0, [c.b_const])
        op_memset(sch, "dve", eps_t[:], EPS, [c.b_const])
        sch.flush()
        stop = DBG["stop_after"]
        done = False
        for l in range(depth):
            xsrc = c.x if l == 0 else c.xs
            for (nm, fn) in (("p1", lambda: phase1(nc, sch, c, l, xsrc)),
                             ("p2", lambda: phase2(nc, sch, c, l)),
                             ("p3", lambda: phase3(nc, sch, c, l))):
                fn()
                if stop == (l, nm):
                    done = True
                    break
            if done:
                break
            with nc.sbuf_tensor(f"W2pre_{l}", [128, 32, D], BF16) as W2pre:
                phase4a(nc, sch, c, l, xsrc, W2pre)
                if stop == (l, "p4a"):
                    done = True
                else:
                    phase4b(nc, sch, c, l, l == depth - 1, W2pre)
                    if stop == (l, "p4b"):
                        done = True
            if done:
                break
    return nc


def make_consts():
    k = {}
    k["k_ident"] = np.eye(128, dtype=np.float32)
    k["k_bigident"] = (30000.0 * np.eye(128)).astype(np.float32)
    p = np.arange(128)[:, None]
    r = np.arange(128)[None, :]
    k["k_maskA"] = np.where(r < p, 0.0, -30000.0).astype(np.float32)
    adm = (p >= 64) | (r < 64)
    k["k_mbdiag"] = np.where(adm, 1.0, 0.0).astype(np.float32)
    init = np.zeros((128, 1024), np.float32)
    init[:, 896:] = np.where(adm, 0.0, -1e30)
    k["k_initdiag"] = init
    sel = np.zeros((40, 2, 128), np.float32)
    for cc in range(2):
        for hh in range(4):
            sel[32 + 4 * cc + hh, cc, hh * 32:(hh + 1) * 32] = 1.0
    k["k_selw"] = sel
    k["k_pow2"] = np.tile((2.0 ** -(np.arange(KBIS) + 1.0))[None, :], (128, 1)).astype(np.float32)
    t = np.arange(S, dtype=np.float32)

    def tables(rot, hd):
        inv = (np.float32(THETA) ** (-(np.arange(0, rot, 2, dtype=np.float32) / np.float32(rot)))).astype(np.float32)
        ang = (t[:, None] * inv[None, :]).astype(np.float32)
        cs, sn = np.cos(ang).astype(np.float32), np.sin(ang).astype(np.float32)
        C = np.ones((128, S), np.float32)
        Sn = np.zeros((128, S), np.float32)
        for pp in range(128):
            i = pp % hd
            if i < rot:
                C[pp] = cs[:, i % (rot // 2)]
                Sn[pp] = sn[:, i % (rot // 2)]
        return C, Sn

    k["k_C64"], k["k_S64"] = tables(16, 64)
    k["k_C32"], k["k_S32"] = tables(8, 32)
    return k


def kernel(x, g_mix, w_in, w_up_a, w_up_b, w_o, g_mlp, w_ff1, w_ff2, g_final):
    f = lambda a: np.ascontiguousarray(np.asarray(a, dtype=np.float32))
    x = f(x)
    shared = dict(g_mix=f(g_mix), w_in=f(w_in), w_up_a=f(w_up_a), w_up_b=f(w_up_b), w_o=f(w_o), g_mlp=f(g_mlp),
                  w_ff1=f(w_ff1), w_ff2=f(w_ff2), g_final=f(g_final))
    shared.update(make_consts())
    nc = build_nc()
    in_maps = [dict(shared, x=x[b]) for b in range(8)]
    res = run_bass_kernel_spmd(nc, in_maps, core_ids=list(range(8)))
    return np.stack([np.asarray(res.results[b]["out"], dtype=np.float32) for b in range(8)], axis=0)
```

```python
import math
from contextlib import ExitStack

import numpy as np
import concourse.bass as bass
import concourse.mybir as mybir
from concourse.bass_utils import run_bass_kernel_spmd

F32 = mybir.dt.float32
BF16 = mybir.dt.bfloat16
AF = mybir.ActivationFunctionType
ALU = mybir.AluOpType
AX = mybir.AxisListType

S = 4096
D = 1024
NT = S // 128
DEPTH = 2
N_IN = 4648
DFF = 4096
EPS = 1e-6
THETA = 500000.0
KBIS = 13
NKEEP = 256

C_QA, C_KA, C_VA, C_QB, C_KB, C_VB, C_QI, C_KI, C_WI, C_G = 0, 512, 1024, 1536, 2048, 2176, 2304, 2560, 2592, 2600
X_QB, X_KB, X_QI, X_KI = 4648, 4648 + 512, 4648 + 640, 4648 + 896
NX = 4648 + 928

DBG = {"stop_after": None, "dump": False}


class Buf:
    __slots__ = ("name", "writer", "rd_eng", "rd_dma")

    def __init__(self, name=""):
        self.name = name
        self.writer = None
        self.rd_eng = {}
        self.rd_dma = []

    def reset(self):
        self.writer = None
        self.rd_eng = {}
        self.rd_dma = []


class Op:
    __slots__ = ("eng", "fn", "deps", "is_dma", "need_inc", "sem", "val", "seq")


ENGS = ["pe", "act", "dve", "pool", "sp"]
NDMA = 24
NDMA_HW = 16


class Sched:
    def __init__(self, nc, block):
        self.nc = nc
        self.block = block
        self.h = {"pe": nc.tensor, "act": nc.scalar, "dve": nc.vector, "pool": nc.gpsimd, "sp": nc.sync}
        self.bstart = {"pe": block.tensor, "act": block.scalar, "dve": block.vector, "pool": block.gpsimd,
                       "sp": block.sync}
        self.psem = {e: nc.alloc_semaphore(f"prog_{e}") for e in ENGS}
        self.pcnt = {e: 0 for e in ENGS}
        self.dsem = [nc.alloc_semaphore(f"dma_{k}") for k in range(NDMA)]
        self.dval = [0] * NDMA
        self.dlast = [None] * NDMA
        self.drr = 0
        self.drr_sw = 0
        self.ops = {e: [] for e in ENGS}
        self.waited = {e: {} for e in ENGS}
        self.bufs = []
        self.seq = 0
        self.last = {e: None for e in ENGS}
        self.phase_dma = []

    def buf(self, name=""):
        b = Buf(name)
        self.bufs.append(b)
        return b

    def bufs_n(self, n, name=""):
        return [self.buf(f"{name}{i}") for i in range(n)]

    def op(self, eng, fn, reads=(), writes=(), dma=False):
        o = Op()
        o.eng, o.fn, o.is_dma, o.need_inc, o.sem, o.val = eng, fn, dma, dma, None, 0
        o.seq = self.seq
        self.seq += 1
        deps = set()
        for b in reads:
            if b.writer is not None:
                deps.add(b.writer)
        for b in writes:
            if b.writer is not None:
                deps.add(b.writer)
            deps.update(b.rd_eng.values())
            deps.update(b.rd_dma)
        if dma:
            if eng == "pool":
                k = NDMA_HW + self.drr_sw
                self.drr_sw = (self.drr_sw + 1) % (NDMA - NDMA_HW)
            else:
                k = self.drr
                self.drr = (self.drr + 1) % NDMA_HW
            if self.dlast[k] is not None:
                deps.add(self.dlast[k])
            self.dval[k] += 16
            o.sem, o.val = self.dsem[k], self.dval[k]
            self.dlast[k] = o
            self.phase_dma.append(o)
        deps.discard(o)
        o.deps = deps
        for b in reads:
            if dma:
                b.rd_dma.append(o)
            else:
                b.rd_eng[eng] = o
        for b in writes:
            b.writer = o
            b.rd_eng = {}
            b.rd_dma = []
        self.ops[eng].append(o)
        if not dma:
            self.last[eng] = o
        return o

    def barrier(self):
        deps = set(x for x in self.last.values() if x is not None)
        deps.update(x for x in self.dlast if x is not None)
        for e in ENGS:
            o = Op()
            o.eng, o.fn, o.is_dma, o.need_inc, o.sem, o.val = e, None, False, False, None, 0
            o.seq = self.seq
            self.seq += 1
            o.deps = set(deps)
            self.ops[e].append(o)

    def flush(self):
        self.barrier()
        for e in ENGS:
            for o in self.ops[e]:
                for d in o.deps:
                    if d.is_dma:
                        continue
                    if d.eng == "pe" and o.eng == "pe" and not o.is_dma:
                        continue
                    d.need_inc = True
        for e in ENGS:
            for o in self.ops[e]:
                if (not o.is_dma) and o.need_inc:
                    self.pcnt[e] += 1
                    o.sem, o.val = self.psem[e], self.pcnt[e]
        for e in ENGS:
            ops = self.ops[e]
            if not ops:
                continue
            waited = self.waited[e]

            def body(eng, ops=ops, waited=waited, e=e):
                for o in ops:
                    need = {}
                    for d in o.deps:
                        if (not d.is_dma) and d.eng == "pe" and e == "pe" and not o.is_dma:
                            continue
                        s = d.sem
                        key = id(s)
                        if key not in need or need[key][1] < d.val:
                            need[key] = (s, d.val)
                    for key, (s, v) in need.items():
                        if waited.get(key, 0) >= v:
                            continue
                        eng.wait_ge(s, v)
                        waited[key] = v
                    if o.fn is None:
                        continue
                    ins = o.fn(eng)
                    if o.is_dma:
                        ins.then_inc(o.sem, 16)
                    elif o.need_inc:
                        ins.then_inc(o.sem, 1)

            self.bstart[e](body)
        self.ops = {e: [] for e in ENGS}
        for b in self.bufs:
            b.reset()
        self.dlast = [None] * NDMA
        self.last = {e: None for e in ENGS}
        self.phase_dma = []


def op_mm(sch, out, lhsT, rhs, start, stop, reads, writes):
    return sch.op("pe", lambda e: e.matmul(out, lhsT=lhsT, rhs=rhs, start=start, stop=stop), reads, writes)


def op_tr(sch, out, in_, ident, reads, writes):
    return sch.op("pe", lambda e: e.transpose(out, in_, ident), reads, writes)


def op_dma(sch, eng, out, in_, reads, writes, **kw):
    return sch.op(eng, lambda e: e.dma_start(out=out, in_=in_, **kw), reads, writes, dma=True)


def op_act(sch, out, in_, func, reads, writes, **kw):
    return sch.op("act", lambda e: e.activation(out=out, in_=in_, func=func, **kw), reads, writes)


def op_ts(sch, eng, out, in0, s1, s2, op0, op1, reads, writes, accum_out=None):
    if op1 is None:
        return sch.op(eng, lambda e: e.tensor_scalar(out=out, in0=in0, scalar1=s1, scalar2=None, op0=op0), reads,
                      writes)
    if accum_out is None:
        return sch.op(eng, lambda e: e.tensor_scalar(out=out, in0=in0, scalar1=s1, scalar2=s2, op0=op0, op1=op1),
                      reads, writes)
    return sch.op(eng, lambda e: e.tensor_scalar(out=out, in0=in0, scalar1=s1, scalar2=s2, op0=op0, op1=op1,
                                                  accum_out=accum_out), reads, writes)


def op_tt(sch, eng, out, in0, in1, op, reads, writes):
    return sch.op(eng, lambda e: e.tensor_tensor(out=out, in0=in0, in1=in1, op=op), reads, writes)


def op_stt(sch, out, in0, scalar, in1, op0, op1, reads, writes, accum_out=None):
    if accum_out is None:
        return sch.op("dve", lambda e: e.scalar_tensor_tensor(out=out, in0=in0, scalar=scalar, in1=in1, op0=op0,
                                                              op1=op1), reads, writes)
    return sch.op("dve", lambda e: e.scalar_tensor_tensor(out=out, in0=in0, scalar=scalar, in1=in1, op0=op0,
                                                          op1=op1, accum_out=accum_out), reads, writes)


def op_copy(sch, eng, out, in_, reads, writes):
    if eng == "act":
        return sch.op("act", lambda e: e.copy(out=out, in_=in_), reads, writes)
    return sch.op(eng, lambda e: e.tensor_copy(out=out, in_=in_), reads, writes)


def op_memset(sch, eng, ap, val, writes):
    return sch.op(eng, lambda e: e.memset(ap, val), (), writes)


class Ctx:
    pass


def rms_rstd(sch, c, xt_ap, b_x, junk_ap, b_junk, sc_ap, b_sc):
    op_stt(sch, junk_ap, xt_ap, 1.0, xt_ap, ALU.mult, ALU.mult, [b_x], [b_junk, b_sc], accum_out=sc_ap[:, 0:1])
    op_act(sch, sc_ap[:, 1:2], sc_ap[:, 0:1], AF.Sqrt, [b_sc], [b_sc], scale=1.0 / D, bias=c.eps_ap)
    sch.op("dve", lambda e: e.reciprocal(sc_ap[:, 2:3], sc_ap[:, 1:2]), [b_sc], [b_sc])


def load_bcast_vec(sch, dst_ap, b_dst, vec_ap):
    op_dma(sch, "sp", dst_ap, vec_ap.partition_broadcast(128), [], [b_dst])


def phase1(nc, sch, c, l, xsrc):
    with ExitStack() as es:
        def sb(name, shape, dt):
            return es.enter_context(nc.sbuf_tensor(f"p1_{l}_{name}", shape, dt))

        Wx = sb("Wx", [128, 8, NX], BF16)
        W_EDGES = (0, 512, 1024, 1536, 2048, 4096, N_IN, NX)
        b_Wr = sch.bufs_n(len(W_EDGES) - 1)
        R_PRIME = len(W_EDGES) - 2

        def wb(col, n):
            return [b_Wr[r] for r in range(len(W_EDGES) - 1) if col < W_EDGES[r + 1] and col + n > W_EDGES[r]]
        gb = sb("gb", [128, D], F32)
        b_gb = sch.buf()
        xt = [sb(f"xt{i}", [128, D], F32) for i in range(2)]
        b_xt = sch.bufs_n(2)
        junk = sb("junk", [128, D], BF16)
        b_junk = sch.buf()
        scr = [sb(f"scr{i}", [128, 4], F32) for i in range(2)]
        b_scr = sch.bufs_n(2)
        hb = [sb(f"hb{i}", [128, D], BF16) for i in range(2)]
        b_hb = sch.bufs_n(2)
        hT = [sb(f"hT{i}", [128, 8, 512], BF16) for i in range(2)]
        b_hT = sch.bufs_n(2)
        tab = [[sb(f"tab{i}_{j}", [128, 512], F32) for j in range(4)] for i in range(2)]
        b_tab = [sch.bufs_n(4) for _ in range(2)]
        fmo = [sb(f"fmo{i}", [128, 512], BF16) for i in range(3)]
        b_fmo = sch.bufs_n(3)
        fmf = [sb(f"fmf{i}", [128, 512], F32) for i in range(2)]
        b_fmf = sch.bufs_n(2)
        tmp = [sb(f"tmp{i}", [128, 512], F32) for i in range(4)]
        b_tmp = sch.bufs_n(4)
        tvo = [sb(f"tvo{i}", [128, 512], BF16) for i in range(2)]
        b_tvo = sch.bufs_n(2)
        gto = [sb(f"gto{i}", [128, 2048], BF16) for i in range(2)]
        b_gto = sch.bufs_n(2)
        kw = [sb(f"kw{i}", [64, 512], F32) for i in range(2)]
        b_kw = sch.bufs_n(2)
        selw = sb("selw", [40, 2, 128], F32)
        b_selw = sch.buf()
        tq = [sb(f"tq{i}", [128, 512], F32) for i in range(2)]
        h32 = [sb(f"h32{i}", [128, 512], F32) for i in range(2)]
        hib = [sb(f"hib{i}", [128, 512], BF16) for i in range(2)]
        lob = [sb(f"lob{i}", [128, 512], BF16) for i in range(2)]
        b_tq, b_h32, b_hib, b_lob = sch.bufs_n(2), sch.bufs_n(2), sch.bufs_n(2), sch.bufs_n(2)
        op_dma(sch, "sp", selw[:], c.k_selw, [], [b_selw])
        rnd = [0]

        def split_store(src_ap, b_src, npart, stores):
            r = rnd[0] % 2
            rnd[0] += 1
            op_copy(sch, "act", hib[r][0:npart, :], src_ap, [b_src], [b_hib[r]])
            op_copy(sch, "pool", h32[r][0:npart, :], hib[r][0:npart, :], [b_hib[r]], [b_h32[r]])
            op_tt(sch, "dve", lob[r][0:npart, :], src_ap, h32[r][0:npart, :], ALU.subtract, [b_src, b_h32[r]],
                  [b_lob[r]])
            for (d_hi, d_lo, r0, r1) in stores:
                op_dma(sch, "sp", d_hi, hib[r][r0:r1, :], [b_hib[r]], [])
                op_dma(sch, "sp", d_lo, lob[r][r0:r1, :], [b_lob[r]], [])

        wsrc = c.w_in[l].rearrange("(k p) n -> p k n", p=128)
        for r_ in range(R_PRIME):
            a, b = W_EDGES[r_], W_EDGES[r_ + 1]
            if b - a <= 512:
                op_dma(sch, "pool", Wx[:, :, a:b], wsrc[:, :, a:b], [], [b_Wr[r_]])
            else:
                for kc in range(8):
                    op_dma(sch, "pool", Wx[:, kc, a:b], wsrc[:, kc, a:b], [], [b_Wr[r_]])
        load_bcast_vec(sch, gb[:], b_gb, c.g_mix[l])
        for (src0, dst0, nh, hd, half) in ((C_QB, X_QB, 8, 64, 8), (C_KB, X_KB, 2, 64, 8), (C_QI, X_QI, 8, 32, 4),
                                           (C_KI, X_KI, 1, 32, 4)):
            w = nh * hd
            op_memset(sch, "dve", Wx[:, :, dst0:dst0 + w], 0.0, [b_Wr[R_PRIME]])
            for kc in range(8):
                sv = Wx[:, kc, src0:src0 + w].rearrange("p (h d) -> p h d", h=nh)
                dv = Wx[:, kc, dst0:dst0 + w].rearrange("p (h d) -> p h d", h=nh)
                op_ts(sch, "dve", dv[:, :, 0:half], sv[:, :, half:2 * half], -1.0, None, ALU.mult, None,
                      wb(src0, w), [b_Wr[R_PRIME]])
                op_copy(sch, "dve", dv[:, :, half:2 * half], sv[:, :, 0:half], wb(src0, w), [b_Wr[R_PRIME]])
        op_ts(sch, "dve", Wx[:, :, C_WI:C_WI + 8], Wx[:, :, C_WI:C_WI + 8], 1.0 / 16.0, None, ALU.mult, None,
              wb(C_WI, 8), wb(C_WI, 8))

        PB = c.PB
        ps = c.ps
        fm_cnt = [0]
        tm_cnt = [0]
        tile_cnt = [0]

        def fm_chunk(g, hTg, b_hTg, wcol, m, pcol, pm, tabk, dst, fp32out, tabs, b_tabs, obuf=None):
            k = fm_cnt[0]
            fm_cnt[0] += 1
            bank_m = 1 + (k % 2)
            bank_p = 3 + (k % 2)
            pm_ap = ps[0:m, bank_m * 512:(bank_m + 1) * 512]
            for kc in range(8):
                op_mm(sch, pm_ap, Wx[:, kc, wcol:wcol + m], hTg[:, kc, :], kc == 0, kc == 7, wb(wcol, m) + [b_hTg],
                      [PB[bank_m]])
            tsl = slice(g * 512, (g + 1) * 512)
            if pcol is None:
                o = fmo[k % 3]
                bo = b_fmo[k % 3]
                op_copy(sch, "act", o[0:m, :], pm_ap, [PB[bank_m]], [bo])
                for (d_ap, r0, r1) in dst:
                    op_dma(sch, "sp", d_ap[:, tsl], o[r0:r1, :], [bo], [])
                return
            pp_ap = ps[0:pm, bank_p * 512:(bank_p + 1) * 512]
            for kc in range(8):
                op_mm(sch, pp_ap, Wx[:, kc, pcol:pcol + pm], hTg[:, kc, :], kc == 0, kc == 7,
                      wb(pcol, pm) + [b_hTg], [PB[bank_p]])
            t1 = tmp[(2 * k) % 4]
            bt1 = b_tmp[(2 * k) % 4]
            t2 = tmp[(2 * k + 1) % 4]
            bt2 = b_tmp[(2 * k + 1) % 4]
            Ct, St = tabs[2 * tabk], tabs[2 * tabk + 1]
            bC, bS = b_tabs[2 * tabk], b_tabs[2 * tabk + 1]
            op_tt(sch, "dve", t1[0:pm, :], pm_ap[0:pm, :], Ct[0:pm, :], ALU.mult, [PB[bank_m], bC], [bt1])
            op_tt(sch, "dve", t2[0:pm, :], pp_ap, St[0:pm, :], ALU.mult, [PB[bank_p], bS], [bt2])
            if obuf is not None:
                o, bo = obuf
            elif fp32out:
                o = fmf[k % 2]
                bo = b_fmf[k % 2]
            else:
                o = fmo[k % 3]
                bo = b_fmo[k % 3]
            op_tt(sch, "pool", o[0:pm, :], t1[0:pm, :], t2[0:pm, :], ALU.add, [bt1, bt2], [bo])
            if m > pm:
                op_copy(sch, "act", o[pm:m, :], pm_ap[pm:m, :], [PB[bank_m]], [bo])
            for (d_ap, r0, r1) in dst:
                op_dma(sch, "sp", d_ap[:, tsl], o[r0:r1, :], [bo], [])
            return o, bo, k

        for g in range(8):
            hTg = hT[g % 2]
            b_hTg = b_hT[g % 2]
            tabs = tab[g % 2]
            b_tabs = b_tab[g % 2]
            tsl = slice(g * 512, (g + 1) * 512)
            for j, tsrc in enumerate((c.c_C64, c.c_S64, c.c_C32, c.c_S32)):
                op_dma(sch, "sp", tabs[j][:], tsrc[:, tsl], [], [b_tabs[j]])
            for j in range(4):
                t = g * 4 + j
                k = tile_cnt[0]
                tile_cnt[0] += 1
                x_t = xt[k % 2]
                bx = b_xt[k % 2]
                sc = scr[k % 2]
                bsc = b_scr[k % 2]
                h_t = hb[k % 2]
                bh = b_hb[k % 2]
                op_dma(sch, "sp", x_t[:], xsrc[t * 128:(t + 1) * 128, :], [], [bx])
                rms_rstd(sch, c, x_t[:], bx, junk[:], b_junk, sc[:], bsc)
                op_stt(sch, h_t[:], x_t[:], sc[:, 2:3], gb[:], ALU.mult, ALU.mult, [bx, bsc, b_gb], [bh])
                tp = ps[:, 0:512].bitcast(BF16)
                for kc in range(8):
                    op_tr(sch, tp[:, kc * 128:(kc + 1) * 128], h_t[:, kc * 128:(kc + 1) * 128], c.ident[:],
                          [bh, c.b_const], [PB[0]])
                op_copy(sch, "act", hTg[:, :, j * 128:(j + 1) * 128], tp.rearrange("p (k t) -> p k t", k=8),
                        [PB[0]], [b_hTg])
            for cc in range(4):
                fm_chunk(g, hTg, b_hTg, C_QA + 128 * cc, 128, None, 0, 0,
                         [(c.qaT[128 * cc:128 * cc + 128], 0, 128)], False, tabs, b_tabs)
            for cc in range(4):
                fm_chunk(g, hTg, b_hTg, C_KA + 128 * cc, 128, None, 0, 0,
                         [(c.kaT[128 * cc:128 * cc + 128], 0, 128)], False, tabs, b_tabs)
            for cc in range(4):
                fm_chunk(g, hTg, b_hTg, C_QB + 128 * cc, 128, X_QB + 128 * cc, 128, 0,
                         [(c.qbT[128 * cc:128 * cc + 128], 0, 128)], False, tabs, b_tabs)
            fm_chunk(g, hTg, b_hTg, C_KB, 128, X_KB, 128, 0, [(c.kbT, 0, 128)], False, tabs, b_tabs)
            kwg, b_kwg = kw[g % 2], b_kw[g % 2]
            fm_chunk(g, hTg, b_hTg, C_KI, 40, X_KI, 32, 1, [], True, tabs, b_tabs, obuf=(kwg, b_kwg))
            split_store(kwg[0:32, :], b_kwg, 32, [(c.kiS[0:32, tsl], c.kiS[32:64, tsl], 0, 32)])
            for cc in range(2):
                o, bo, kk = fm_chunk(g, hTg, b_hTg, C_QI + 128 * cc, 128, X_QI + 128 * cc, 128, 1, [], True, tabs,
                                     b_tabs)
                kx = fm_cnt[0]
                fm_cnt[0] += 1
                bank_w = 1 + (kx % 2)
                wrep = ps[:, bank_w * 512:(bank_w + 1) * 512]
                op_mm(sch, wrep, selw[:, cc, :], kwg[0:40, :], True, True, [b_selw, b_kwg], [PB[bank_w]])
                for sg, alu in enumerate((ALU.max, ALU.min)):
                    r = rnd[0] % 2
                    op_stt(sch, tq[r][:, :], wrep, 0.0, o[:, :], alu, ALU.mult, [PB[bank_w], bo], [b_tq[r]])
                    stores = []
                    for hh in range(4):
                        a = 2 * (4 * cc + hh) + sg
                        stores.append((c.qsS[a, 0:32, tsl], c.qsS[a, 32:64, tsl], hh * 32, (hh + 1) * 32))
                    split_store(tq[r][:, :], b_tq[r], 128, stores)
            for j in range(4):
                t = g * 4 + j
                rows = slice(t * 128, (t + 1) * 128)
                lhs = [hTg[:, kc, j * 128:(j + 1) * 128] for kc in range(8)]
                gt = gto[t % 2]
                bg = b_gto[t % 2]
                for (wcol, n, kind) in ((C_VA, 512, "va"), (C_VB, 128, "vb"), (C_G, 512, 0), (C_G + 512, 512, 1),
                                        (C_G + 1024, 512, 2), (C_G + 1536, 512, 3)):
                    k = tm_cnt[0]
                    tm_cnt[0] += 1
                    bank = 5 + (k % 3)
                    p_ap = ps[:, bank * 512:bank * 512 + n]
                    for kc in range(8):
                        op_mm(sch, p_ap, lhs[kc], Wx[:, kc, wcol:wcol + n], kc == 0, kc == 7,
                              wb(wcol, n) + [b_hTg], [PB[bank]])
                    if kind == "va" or kind == "vb":
                        o = tvo[k % 2]
                        bo = b_tvo[k % 2]
                        op_copy(sch, "act", o[:, 0:n], p_ap, [PB[bank]], [bo])
                        dstt = c.va if kind == "va" else c.vb
                        op_dma(sch, "sp", dstt[rows, :], o[:, 0:n], [bo], [])
                    else:
                        op_act(sch, gt[:, kind * 512:(kind + 1) * 512], p_ap, AF.Sigmoid, [PB[bank]], [bg])
                op_dma(sch, "sp", c.gates[rows, :], gt[:], [bg], [])
        sch.flush()


def phase2(nc, sch, c, l):
    CW = 1024
    with ExitStack() as es:
        def sb(name, shape, dt):
            return es.enter_context(nc.sbuf_tensor(f"p2_{l}_{name}", shape, dt))

        QT = [sb(f"QT{i}", [64, S], BF16) for i in range(2)]
        KT = [sb(f"KT{i}", [64, S], BF16) for i in range(2)]
        b_QK = sch.bufs_n(2)
        V = sb("V", [128, NT, 512], BF16)
        b_V = sch.buf()
        om = [sb(f"om{i}", [128, S], F32) for i in range(2)]
        b_om = [sch.bufs_n(S // CW) for _ in range(2)]
        P = [sb(f"P{i}", [128, S + 4], F32) for i in range(2)]
        b_P = [sch.bufs_n(S // CW + 1) for _ in range(2)]
        NA = 6
        A = [sb(f"A{i}", [128, CW], BF16) for i in range(NA)]
        b_A = sch.bufs_n(NA)
        AT = [sb(f"AT{i}", [128, CW], BF16) for i in range(3)]
        b_AT = sch.bufs_n(3)
        ya = sb("ya", [128, NT, 512], BF16)
        b_ya = sch.buf()
        PB = c.PB
        ps = c.ps

        vsrc = c.va.rearrange("(n p) d -> p n d", p=128)
        for q in range(4):
            op_dma(sch, "sp", V[:, q * 8:(q + 1) * 8, :], vsrc[:, q * 8:(q + 1) * 8, :], [], [b_V])

        def load_head(h):
            s = h % 2
            op_dma(sch, "sp", QT[s][:], c.qaT[h * 64:(h + 1) * 64, :], [], [b_QK[s]])
            op_dma(sch, "sp", KT[s][:], c.kaT[h * 64:(h + 1) * 64, :], [], [b_QK[s]])

        jobs = []
        for h in range(8):
            for i in range(NT):
                n = (i + 1) * 128
                nch = (n + CW - 1) // CW
                for cc in reversed(range(nch)):
                    jobs.append((h, i, cc, nch))
        LAG = 3
        zc = [0]
        tc_ = [0]

        def stage1(j):
            h, i, cc, nch = jobs[j]
            n = (i + 1) * 128
            c0 = cc * CW
            ln = min(CW, n - c0)
            s = h % 2
            slot = (h * NT + i) % 2
            first = (cc == nch - 1)
            if first and i == 0:
                if h + 1 < 8:
                    load_head(h + 1)
            if first:
                op_memset(sch, "dve", P[slot][:, n:n + 1], 1.0, [b_P[slot][n // CW]])
            zb = 2 * (zc[0] % 2)
            zc[0] += 1
            zw = [PB[zb], PB[zb + 1]]
            q_ap = QT[s][:, i * 128:(i + 1) * 128]
            for sub in range(0, ln, 512):
                sln = min(512, ln - sub)
                lastsub = (sub + sln == ln)
                op_mm(sch, ps[:, zb * 512 + sub:zb * 512 + sub + sln], q_ap, KT[s][:, c0 + sub:c0 + sub + sln], True,
                      not (first and lastsub), [b_QK[s]], zw)
            if first:
                op_mm(sch, ps[:, zb * 512 + ln - 128:zb * 512 + ln], c.ident[:], c.maskA[:], False, True,
                      [c.b_const], zw)
            op_act(sch, om[slot][:, c0:c0 + ln], ps[:, zb * 512:zb * 512 + ln], AF.Sigmoid, zw, [b_om[slot][cc]],
                   scale=-0.125)
            nxt = (c0 + ln) // CW
            sch.op("dve", lambda e: e.tensor_tensor_scan(out=P[slot][:, c0:c0 + ln][:, ::-1],
                                                         data0=om[slot][:, c0:c0 + ln][:, ::-1],
                                                         data1=om[slot][:, c0:c0 + ln][:, ::-1],
                                                         initial=P[slot][:, c0 + ln:c0 + ln + 1],
                                                         op0=ALU.mult, op1=ALU.bypass),
                   [b_om[slot][cc], b_P[slot][nxt]], [b_P[slot][cc]])
            a = A[j % NA]
            op_tt(sch, "pool", a[:, 0:ln], P[slot][:, c0 + 1:c0 + ln + 1], P[slot][:, c0:c0 + ln], ALU.subtract,
                  [b_P[slot][cc], b_P[slot][nxt]], [b_A[j % NA]])

        def stage2(j):
            h, i, cc, nch = jobs[j]
            n = (i + 1) * 128
            c0 = cc * CW
            ln = min(CW, n - c0)
            nb = ln // 128
            a = A[j % NA]
            k = tc_[0]
            tc_[0] += 1
            tbank = (4, 7)[k % 2]
            tp = ps[:, tbank * 512:tbank * 512 + 512].bitcast(BF16)
            for b in range(nb):
                op_tr(sch, tp[:, b * 128:(b + 1) * 128], a[:, b * 128:(b + 1) * 128], c.ident[:],
                      [b_A[j % NA], c.b_const], [PB[tbank]])
            at = AT[k % 3]
            op_copy(sch, "act", at[:, 0:ln], tp[:, 0:ln], [PB[tbank]], [b_AT[k % 3]])
            ybank = 5 + ((h * NT + i) % 2)
            y = ps[:, ybank * 512:ybank * 512 + 64]
            for b in range(nb):
                kb = (c0 // 128) + b
                first = (cc == nch - 1) and b == 0
                last = (cc == 0) and b == nb - 1
                op_mm(sch, y, at[:, b * 128:(b + 1) * 128], V[:, kb, h * 64:(h + 1) * 64], first, last,
                      [b_AT[k % 3], b_V], [PB[ybank]])
            if cc == 0:
                op_copy(sch, "act", ya[:, i, h * 64:(h + 1) * 64], y, [PB[ybank]], [b_ya])

        load_head(0)
        J = len(jobs)
        for j in range(J + LAG):
            if j < J:
                stage1(j)
            if j - LAG >= 0:
                stage2(j - LAG)
        ydst = c.ya.rearrange("(n p) d -> p n d", p=128)
        for q in range(4):
            op_dma(sch, "sp", ydst[:, q * 8:(q + 1) * 8, :], ya[:, q * 8:(q + 1) * 8, :], [b_ya], [])
        sch.flush()


def phase3(nc, sch, c, l):
    with ExitStack() as es:
        def sb(name, shape, dt):
            return es.enter_context(nc.sbuf_tensor(f"p3_{l}_{name}", shape, dt))

        kiR = sb("kiR", [96, S], BF16)
        b_ki = sch.buf()
        qst = [sb(f"qst{i}", [96, 16, 128], BF16) for i in range(2)]
        b_qst = sch.bufs_n(2)
        d_qsS = sch.buf()
        d_kiS = sch.buf()
        qbT = sb("qbT", [128, 4, S], BF16)
        b_qb = sch.buf()
        kbR = sb("kbR", [128, 2, S], BF16)
        b_kb = sch.buf()
        Vb = sb("Vb", [128, NT, 2, 65], BF16)
        b_V = sch.buf()
        ybu = [sb(f"ybu{i}", [128, 8, 65], F32) for i in range(2)]
        b_ybu = sch.bufs_n(2)
        r8 = [sb(f"r8{i}", [128, 8], F32) for i in range(2)]
        m8 = [sb(f"m8{i}", [128, 32, 8], F32) for i in range(2)]
        acc = [sb(f"acc{i}", [128, S], F32) for i in range(2)]
        b_acc = [sch.bufs_n(8) for _ in range(2)]
        mb = [sb(f"mb{i}", [128, S], BF16) for i in range(2)]
        b_mb = sch.bufs_n(2)
        jk = sb("jk", [128, S], BF16)
        b_jk = sch.buf()
        bs = [sb(f"bs{i}", [128, 8], F32) for i in range(2)]
        b_bs = sch.bufs_n(2)
        steps = [sb(f"steps{i}", [128, KBIS], F32) for i in range(2)]
        NE = 7
        E = [sb(f"E{i}", [128, 512], BF16) for i in range(3)]
        b_E = sch.bufs_n(3)
        EM = [sb(f"EM{i}", [128, 512], BF16) for i in range(NE)]
        b_EM = sch.bufs_n(NE)
        MT = [sb(f"MT{i}", [128, S], BF16) for i in range(2)]
        b_MT = sch.bufs_n(2)
        rsp = [sb(f"rsp{i}", [128, 12], F32) for i in range(4)]
        b_rsp = sch.bufs_n(4)
        ybt = [sb(f"ybt{i}", [128, 512], BF16) for i in range(2)]
        b_ybt = sch.bufs_n(2)
        PB = c.PB
        ps = c.ps

        op_dma(sch, "sp", kiR[0:32, :], c.kiS[0:32, :], [], [b_ki])
        op_dma(sch, "sp", kiR[32:64, :], c.kiS[0:32, :], [], [b_ki])
        op_dma(sch, "sp", kiR[64:96, :], c.kiS[32:64, :], [], [b_ki])
        qsrc = c.qbT.rearrange("(k p) t -> p k t", p=128)
        for k in range(4):
            op_dma(sch, "sp", qbT[:, k, :], qsrc[:, k, :], [], [b_qb])
        for g in range(2):
            for r in range(2):
                op_dma(sch, "sp", kbR[r * 64:(r + 1) * 64, g, :], c.kbT[g * 64:(g + 1) * 64, :], [], [b_kb])
        vsrc = c.vb.rearrange("(n p) d -> p n d", p=128)
        op_memset(sch, "pool", Vb[:, :, :, 64:65], 1.0, [b_V])
        for q in range(4):
            for g in range(2):
                op_dma(sch, "sp", Vb[:, q * 8:(q + 1) * 8, g, 0:64], vsrc[:, q * 8:(q + 1) * 8, g * 64:(g + 1) * 64],
                       [], [b_V])
        sc_cnt = [0]

        def score_units(i):
            units = []
            n = (i + 1) * 128
            nch2 = (n + 1023) // 1024
            sl = i % 2

            def load():
                op_dma(sch, "sp", qst[sl][0:64, :, :],
                       c.qsS[:, :, i * 128:(i + 1) * 128].rearrange("a r t -> r a t"), [d_qsS], [b_qst[sl]])
                op_dma(sch, "sp", qst[sl][64:96, :, :],
                       c.qsS[:, 0:32, i * 128:(i + 1) * 128].rearrange("a r t -> r a t"), [d_qsS], [b_qst[sl]])

            def pair(cc, a_idx, first):
                c0 = cc * 1024
                ln = min(1024, n - c0)
                a_ap = acc[sl][:, c0:c0 + ln]
                ab = b_acc[sl][2 * cc:2 * cc + (2 if ln > 512 else 1)]
                diag = (cc == nch2 - 1)
                alu = ALU.max if (a_idx % 2 == 0) else ALU.min
                k = sc_cnt[0]
                sc_cnt[0] += 1
                zb = (2, 6)[k % 2]
                zw = [PB[zb], PB[zb + 1]]
                for sub in range(0, ln, 512):
                    sln = min(512, ln - sub)
                    op_mm(sch, ps[:, zb * 512 + sub:zb * 512 + sub + sln], qst[sl][:, a_idx, :],
                          kiR[:, c0 + sub:c0 + sub + sln], True, True, [b_qst[sl], b_ki], zw)
                y = ps[:, zb * 512:zb * 512 + ln]
                if first:
                    in1 = c.initdiag[:, 1024 - ln:1024] if diag else c.zeros[:, 0:ln]
                    op_stt(sch, a_ap, y, 0.0, in1, alu, ALU.add, zw + [c.b_const], ab)
                else:
                    op_stt(sch, a_ap, y, 0.0, a_ap, alu, ALU.add, zw + ab, ab)

            for cc in range(nch2):
                for a_idx in range(16):
                    if cc == 0 and a_idx == 0:
                        units.append(lambda: (load(), pair(0, 0, True)))
                    else:
                        units.append((lambda cc, a_idx: (lambda: pair(cc, a_idx, a_idx == 0)))(cc, a_idx))
            return units

        mt_cnt = [0]

        def mask_T(i):
            n = (i + 1) * 128
            sl = i % 2
            for c0 in range(0, n, 512):
                ln = min(512, n - c0)
                k = mt_cnt[0]
                mt_cnt[0] += 1
                tbank = 2
                tp = ps[:, tbank * 512:tbank * 512 + 512].bitcast(BF16)
                for b in range(ln // 128):
                    op_tr(sch, tp[:, b * 128:(b + 1) * 128], mb[sl][:, c0 + b * 128:c0 + (b + 1) * 128], c.ident[:],
                          [b_mb[sl], c.b_const], [PB[tbank]])
                op_copy(sch, "act", MT[sl][:, c0:c0 + ln], tp[:, 0:ln], [PB[tbank]], [b_MT[sl]])

        def thresh_steps(i):
            n = (i + 1) * 128
            nch = (n + 511) // 512
            sl = i % 2
            m = mb[sl]
            if i < 2:
                def const_mask():
                    if n > 128:
                        op_memset(sch, "pool", m[:, 0:n - 128], 1.0, [b_mb[sl]])
                    op_copy(sch, "pool", m[:, n - 128:n], c.mbdiag[:], [c.b_const], [b_mb[sl]])
                    mask_T(i)
                return [const_mask]
            a = acc[sl]
            ba = b_acc[sl][0:nch]
            b = bs[sl]
            bb = b_bs[sl]
            st = steps[sl]
            mm8 = m8[sl]
            seg = (n - 128) // 32

            def init():
                for jj in range(32):
                    sch.op("dve", (lambda jj: (lambda e: e.max(out=mm8[:, jj, :],
                                                               in_=a[:, jj * seg:(jj + 1) * seg])))(jj), ba, [bb])
                sch.op("dve", lambda e: e.tensor_reduce(out=b[:, 0:1], in_=mm8[:, :, 7], axis=AX.X, op=ALU.min), [bb],
                       [bb])
                sch.op("dve", lambda e: e.tensor_reduce(out=b[:, 6:7], in_=mm8[:, :, 7], axis=AX.X, op=ALU.max), [bb],
                       [bb])
                sch.op("dve", lambda e: e.tensor_reduce(out=b[:, 7:8], in_=a[:, n - 128:n], axis=AX.X, op=ALU.max),
                       ba, [bb])
                op_tt(sch, "dve", b[:, 1:2], b[:, 6:7], b[:, 7:8], ALU.max, [bb], [bb])
                op_tt(sch, "dve", b[:, 2:3], b[:, 1:2], b[:, 0:1], ALU.subtract, [bb], [bb])
                op_ts(sch, "dve", st[:, :], c.pow2[:, :], b[:, 2:3], None, ALU.mult, None, [bb, c.b_const], [bb])
                op_ts(sch, "dve", b[:, 3:4], b[:, 0:1], st[:, 0:1], -1.0, ALU.add, ALU.mult, [bb], [bb])

            def it_a(k):
                op_act(sch, jk[:, 0:n], a[:, 0:n], AF.Sign, ba + [bb], [b_jk, bb], bias=b[:, 3:4], scale=1.0,
                       accum_out=b[:, 4:5])

            def it_b(k):
                op_ts(sch, "dve", b[:, 5:6], b[:, 4:5], 511.0 - n, st[:, k:k + 1], ALU.is_ge, ALU.mult, [bb], [bb])
                op_tt(sch, "dve", b[:, 0:1], b[:, 0:1], b[:, 5:6], ALU.add, [bb], [bb])
                if k + 1 < KBIS:
                    op_ts(sch, "dve", b[:, 3:4], b[:, 0:1], st[:, k + 1:k + 2], -1.0, ALU.add, ALU.mult, [bb], [bb])

            def fin():
                op_ts(sch, "dve", m[:, 0:n], a[:, 0:n], b[:, 0:1], None, ALU.is_ge, None, ba + [bb], [b_mb[sl]])
                mask_T(i)

            steps_ = [init]
            for k in range(KBIS):
                steps_.append((lambda k: (lambda: it_a(k)))(k))
                steps_.append((lambda k: (lambda: it_b(k)))(k))
            return steps_ + [fin]

        jobs = []
        order = list(range(NT - 1, -1, -1))
        for i in order:
            n = (i + 1) * 128
            nch = (n + 511) // 512
            for h in range(8):
                for cc in range(nch):
                    jobs.append((i, h, cc, nch))
        LAG = 5
        zc = [0]
        tc_ = [0]

        def stage1(j):
            i, h, cc, nch = jobs[j]
            n = (i + 1) * 128
            c0 = cc * 512
            ln = min(512, n - c0)
            nb = ln // 128
            sl = i % 2
            g = h // 4
            pb = (h % 2) * 64
            bank = zc[0] % 2
            zc[0] += 1
            q_ap = qbT[pb:pb + 64, h // 2, i * 128:(i + 1) * 128]
            for b in range(nb):
                op_mm(sch, ps[:, bank * 512 + b * 128:bank * 512 + (b + 1) * 128],
                      kbR[pb:pb + 64, g, c0 + b * 128:c0 + (b + 1) * 128], q_ap, True, True, [b_qb, b_kb],
                      [PB[bank]])
            e_ = E[j % 3]
            op_act(sch, e_[:, 0:ln], ps[:, bank * 512:bank * 512 + ln], AF.Exp, [PB[bank]], [b_E[j % 3]], scale=0.125)
            op_tt(sch, "pool", EM[j % NE][:, 0:ln], e_[:, 0:ln], MT[sl][:, c0:c0 + ln], ALU.mult,
                  [b_E[j % 3], b_MT[sl]], [b_EM[j % NE]])

        def stage2(j):
            i, h, cc, nch = jobs[j]
            n = (i + 1) * 128
            c0 = cc * 512
            ln = min(512, n - c0)
            nb = ln // 128
            g = h // 4
            em = EM[j % NE]
            ybank = 4 + ((i * 8 + h) % 2)
            y = ps[:, ybank * 512:ybank * 512 + 65]
            for b in range(nb):
                kb = (c0 // 128) + b
                op_mm(sch, y, em[:, b * 128:(b + 1) * 128], Vb[:, kb, g, :], cc == 0 and b == 0,
                      cc == nch - 1 and b == nb - 1, [b_EM[j % NE], b_V], [PB[ybank]])
            if cc == nch - 1:
                yu = ybu[i % 2]
                byu = b_ybu[i % 2]
                op_copy(sch, "act", yu[:, h, :], y, [PB[ybank]], [byu])
                if h == 7:
                    rr8 = r8[i % 2]
                    yt = ybt[i % 2]
                    sch.op("dve", lambda e: e.reciprocal(rr8[:, :], yu[:, :, 64]), [byu], [byu])
                    for hh in range(8):
                        op_ts(sch, "dve", yt[:, hh * 64:(hh + 1) * 64], yu[:, hh, 0:64], rr8[:, hh:hh + 1], None,
                              ALU.mult, None, [byu], [b_ybt[i % 2]])
                    op_dma(sch, "sp", c.yb[i * 128:(i + 1) * 128, :], yt[:], [b_ybt[i % 2]], [])

        starts = {}
        ends = {}
        for j, (i, h, cc, nch) in enumerate(jobs):
            starts.setdefault(i, j)
            ends[i] = j + 1
        J = len(jobs)

        t0, t1 = order[0], order[1]
        for u in score_units(t0):
            u()
        thr0 = thresh_steps(t0)
        scu1 = score_units(t1)
        si = 0
        for ti, u in enumerate(thr0):
            u()
            s_end = min(len(scu1), ((ti + 1) * len(scu1)) // len(thr0))
            while si < s_end:
                scu1[si]()
                si += 1
        while si < len(scu1):
            scu1[si]()
            si += 1
        for p, i in enumerate(order):
            thr = thresh_steps(order[p + 1]) if p + 1 < NT else []
            scu = []
            if p + 2 < NT and order[p + 2] >= 2:
                scu = score_units(order[p + 2])
            nj = ends[i] - starts[i]
            ti = si = 0
            for jj, j in enumerate(range(starts[i], ends[i])):
                stage1(j)
                if j - LAG >= 0:
                    stage2(j - LAG)
                t_end = min(len(thr), ((jj + 1) * len(thr)) // max(1, int(0.8 * nj)))
                s_end = min(len(scu), ((jj + 1) * len(scu)) // max(1, int(0.9 * nj)))
                while si < s_end or ti < t_end:
                    if si < s_end:
                        scu[si]()
                        si += 1
                    if ti < t_end:
                        thr[ti]()
                        ti += 1
        for j in range(J - LAG, J):
            stage2(j)
        sch.flush()


def phase4a(nc, sch, c, l, xsrc, W2pre=None):
    with ExitStack() as es:
        def sb(name, shape, dt):
            return es.enter_context(nc.sbuf_tensor(f"p4a_{l}_{name}", shape, dt))

        Wua = sb("Wua", [128, 4, D], BF16)
        Wub = sb("Wub", [128, 4, D], BF16)
        Wo = sb("Wo", [128, 8, D], BF16)
        b_Wu = sch.buf()
        b_Wo = sch.buf()
        yt = [sb(f"yt{i}", [128, 1024], BF16) for i in range(3)]
        b_yt = sch.bufs_n(3)
        gt = [sb(f"gt{i}", [128, 2048], BF16) for i in range(3)]
        b_gt = sch.bufs_n(3)
        xt = [sb(f"xt{i}", [128, D], F32) for i in range(3)]
        b_xt = sch.bufs_n(3)
        yT = [sb(f"yT{i}", [128, 8, 128], BF16) for i in range(2)]
        b_yT = sch.bufs_n(2)
        m1 = [sb(f"m1_{i}", [128, D], F32) for i in range(2)]
        m2 = [sb(f"m2_{i}", [128, D], F32) for i in range(2)]
        b_m1 = sch.bufs_n(2)
        b_m2 = sch.bufs_n(2)
        mg = [sb(f"mg{i}", [128, D], BF16) for i in range(2)]
        b_mg = sch.bufs_n(2)
        mT = [sb(f"mT{i}", [128, 8, 128], BF16) for i in range(2)]
        b_mT = sch.bufs_n(2)
        PB = c.PB
        ps = c.ps
        for (dstw, srcw, nk) in ((Wua, c.w_up_a[l], 4), (Wub, c.w_up_b[l], 4), (Wo, c.w_o[l], 8)):
            sv = srcw.rearrange("(k p) n -> p k n", p=128)
            for kc in range(nk):
                op_dma(sch, "pool", dstw[:, kc, :], sv[:, kc, :], [], [b_Wo if nk == 8 else b_Wu])
        if W2pre is not None:
            b_pre = sch.buf()
            s2 = c.w_ff2[l].rearrange("(k p) n -> p k n", p=128)
            for q in range(8):
                op_dma(sch, "pool", W2pre[:, q * 4:(q + 1) * 4, :], s2[:, q * 4:(q + 1) * 4, :], [], [b_pre])

        def loads(t):
            s3 = t % 3
            rows = slice(t * 128, (t + 1) * 128)
            op_dma(sch, "sp", yt[s3][:, 0:512], c.ya[rows, :], [], [b_yt[s3]])
            op_dma(sch, "sp", yt[s3][:, 512:1024], c.yb[rows, :], [], [b_yt[s3]])
            op_dma(sch, "sp", gt[s3][:], c.gates[rows, :], [], [b_gt[s3]])
            op_dma(sch, "sp", xt[s3][:], xsrc[rows, :], [], [b_xt[s3]])

        def stage_a(t):
            s = t % 2
            s3 = t % 3
            tp = ps[:, 0:512].bitcast(BF16)
            for kc in range(8):
                op_tr(sch, tp[:, kc * 128:(kc + 1) * 128], yt[s3][:, kc * 128:(kc + 1) * 128], c.ident[:],
                      [b_yt[s3], c.b_const], [PB[0]])
            op_copy(sch, "act", yT[s][:], tp.rearrange("p (k t) -> p k t", k=8), [PB[0]], [b_yT[s]])
            for (W_, koff, bank0) in ((Wua, 0, 1), (Wub, 4, 3)):
                for sl in range(2):
                    p_ap = ps[:, (bank0 + sl) * 512:(bank0 + sl + 1) * 512]
                    for kc in range(4):
                        op_mm(sch, p_ap, yT[s][:, koff + kc, :], W_[:, kc, sl * 512:(sl + 1) * 512], kc == 0, kc == 3,
                              [b_yT[s], b_Wu], [PB[bank0 + sl]])
            op_tt(sch, "dve", m1[s][:], ps[:, 512:1536], gt[s3][:, 0:1024], ALU.mult, [PB[1], PB[2], b_gt[s3]],
                  [b_m1[s]])
            op_tt(sch, "dve", m2[s][:], ps[:, 1536:2560], gt[s3][:, 1024:2048], ALU.mult, [PB[3], PB[4], b_gt[s3]],
                  [b_m2[s]])
            op_tt(sch, "pool", mg[s][:], m1[s][:], m2[s][:], ALU.add, [b_m1[s], b_m2[s]], [b_mg[s]])

        def stage_b(t):
            s = t % 2
            s3 = t % 3
            rows = slice(t * 128, (t + 1) * 128)
            tp2 = ps[:, 5 * 512:6 * 512].bitcast(BF16)
            for kc in range(8):
                op_tr(sch, tp2[:, kc * 128:(kc + 1) * 128], mg[s][:, kc * 128:(kc + 1) * 128], c.ident[:],
                      [b_mg[s], c.b_const], [PB[5]])
            op_copy(sch, "act", mT[s][:], tp2.rearrange("p (k t) -> p k t", k=8), [PB[5]], [b_mT[s]])
            for sl in range(2):
                p_ap = ps[:, (6 + sl) * 512:(7 + sl) * 512]
                for kc in range(8):
                    op_mm(sch, p_ap, mT[s][:, kc, :], Wo[:, kc, sl * 512:(sl + 1) * 512], kc == 0, kc == 7,
                          [b_mT[s], b_Wo], [PB[6 + sl]])
            op_tt(sch, "dve", xt[s3][:], xt[s3][:], ps[:, 6 * 512:8 * 512], ALU.add, [b_xt[s3], PB[6], PB[7]],
                  [b_xt[s3]])
            op_dma(sch, "sp", c.xs[rows, :], xt[s3][:], [b_xt[s3]], [])

        loads(0)
        for t in range(NT + 1):
            if t + 1 < NT:
                loads(t + 1)
            if t < NT:
                stage_a(t)
            if t >= 1:
                stage_b(t - 1)
        sch.flush()


def phase4b(nc, sch, c, l, last, W2pre=None):
    G = 256
    with ExitStack() as es:
        def sb(name, shape, dt):
            return es.enter_context(nc.sbuf_tensor(f"p4b_{l}_{name}", shape, dt))

        W1 = sb("W1", [128, 8, DFF], BF16)
        W2 = W2pre if W2pre is not None else sb("W2", [128, 32, D], BF16)
        b_W1 = sch.bufs_n(8)
        b_W2 = sch.buf()
        gb = sb("gb", [128, D], F32)
        gf = sb("gf", [128, D], F32)
        b_gb = sch.buf()
        xt = [sb(f"xt{i}", [128, D], F32) for i in range(4)]
        b_xt = sch.bufs_n(4)
        junk = sb("junk", [128, D], BF16)
        b_junk = sch.buf()
        scr = [sb(f"scr{i}", [128, 4], F32) for i in range(2)]
        b_scr = sch.bufs_n(2)
        hb = [sb(f"hb{i}", [128, D], BF16) for i in range(2)]
        b_hb = sch.bufs_n(2)
        hT = [sb(f"hT{i}", [128, 8, G], BF16) for i in range(2)]
        b_hT = sch.bufs_n(2)
        rr = [sb(f"rr{i}", [128, G], F32) for i in range(2)]
        b_rr = sch.bufs_n(2)
        uT = sb("uT", [128, 32, G], BF16)
        b_uT = sch.buf()
        ot = [sb(f"ot{i}", [128, D], F32) for i in range(2)]
        b_ot = sch.bufs_n(2)
        PB = c.PB
        ps = c.ps
        s1 = c.w_ff1[l].rearrange("(k p) n -> p k n", p=128)
        for blk in range(8):
            op_dma(sch, "pool", W1[:, :, blk * 512:(blk + 1) * 512], s1[:, :, blk * 512:(blk + 1) * 512], [],
                   [b_W1[blk]])
        if W2pre is None:
            s2 = c.w_ff2[l].rearrange("(k p) n -> p k n", p=128)
            for q in range(8):
                op_dma(sch, "pool", W2[:, q * 4:(q + 1) * 4, :], s2[:, q * 4:(q + 1) * 4, :], [], [b_W2])
        load_bcast_vec(sch, gb[:], b_gb, c.g_mlp[l])
        if last:
            load_bcast_vec(sch, gf[:], b_gb, c.g_final)
        ntg = G // 128
        tcnt = [0]
        fcnt = [0]
        ocnt = [0]
        for g in range(S // G):
            hTg = hT[g % 2]
            bhT = b_hT[g % 2]
            xts = []
            for j in range(ntg):
                t = g * ntg + j
                k = tcnt[0]
                tcnt[0] += 1
                x_t = xt[k % 4]
                bx = b_xt[k % 4]
                xts.append((x_t, bx, t))
                sc = scr[k % 2]
                bsc = b_scr[k % 2]
                h_t = hb[k % 2]
                bh = b_hb[k % 2]
                op_dma(sch, "sp", x_t[:], c.xs[t * 128:(t + 1) * 128, :], [], [bx])
                rms_rstd(sch, c, x_t[:], bx, junk[:], b_junk, sc[:], bsc)
                op_stt(sch, h_t[:], x_t[:], sc[:, 2:3], gb[:], ALU.mult, ALU.mult, [bx, bsc, b_gb], [bh])
                tp = ps[:, 0:512].bitcast(BF16)
                for kc in range(8):
                    op_tr(sch, tp[:, kc * 128:(kc + 1) * 128], h_t[:, kc * 128:(kc + 1) * 128], c.ident[:],
                          [bh, c.b_const], [PB[0]])
                op_copy(sch, "act", hTg[:, :, j * 128:(j + 1) * 128], tp.rearrange("p (k t) -> p k t", k=8), [PB[0]],
                        [bhT])
            for fc in range(32):
                k = fcnt[0]
                fcnt[0] += 1
                bank = 1 + (k % 3)
                p_ap = ps[:, bank * 512:bank * 512 + G]
                for kc in range(8):
                    op_mm(sch, p_ap, W1[:, kc, fc * 128:(fc + 1) * 128], hTg[:, kc, :], kc == 0, kc == 7,
                          [b_W1[fc // 4], bhT], [PB[bank]])
                r = rr[k % 2]
                op_act(sch, r[:], p_ap, AF.Relu, [PB[bank]], [b_rr[k % 2]])
                op_tt(sch, "pool" if (k % 2) else "dve", uT[:, fc, :], r[:], r[:], ALU.mult, [b_rr[k % 2]], [b_uT])
            for j in range(ntg):
                x_t, bx, t = xts[j]
                k = ocnt[0]
                ocnt[0] += 1
                bank0 = 4 + 2 * (k % 2)
                for sl in range(2):
                    p_ap = ps[:, (bank0 + sl) * 512:(bank0 + sl + 1) * 512]
                    for fc in range(32):
                        op_mm(sch, p_ap, uT[:, fc, j * 128:(j + 1) * 128], W2[:, fc, sl * 512:(sl + 1) * 512],
                              fc == 0, fc == 31, [b_uT, b_W2], [PB[bank0 + sl]])
                op_tt(sch, "dve", x_t[:], x_t[:], ps[:, bank0 * 512:(bank0 + 2) * 512], ALU.add,
                      [bx, PB[bank0], PB[bank0 + 1]], [bx])
                rows = slice(t * 128, (t + 1) * 128)
                if not last:
                    op_dma(sch, "sp", c.xs[rows, :], x_t[:], [bx], [])
                else:
                    sc = scr[k % 2]
                    bsc = b_scr[k % 2]
                    o_t = ot[k % 2]
                    bo = b_ot[k % 2]
                    rms_rstd(sch, c, x_t[:], bx, junk[:], b_junk, sc[:], bsc)
                    op_stt(sch, o_t[:], x_t[:], sc[:, 2:3], gf[:], ALU.mult, ALU.mult, [bx, bsc, b_gb], [bo])
                    op_dma(sch, "sp", c.out[rows, :], o_t[:], [bo], [])
        sch.flush()


def build_nc(depth=DEPTH):
    nc = bass.Bass("TRN2", target_bir_lowering=False)
    c = Ctx()

    def din(name, shape):
        return nc.dram_tensor(name, shape, F32, kind="ExternalInput").ap()

    c.x = din("x", [S, D])
    c.g_mix = din("g_mix", [DEPTH, D])
    c.w_in = din("w_in", [DEPTH, D, N_IN])
    c.w_up_a = din("w_up_a", [DEPTH, 512, D])
    c.w_up_b = din("w_up_b", [DEPTH, 512, D])
    c.w_o = din("w_o", [DEPTH, D, D])
    c.g_mlp = din("g_mlp", [DEPTH, D])
    c.w_ff1 = din("w_ff1", [DEPTH, D, DFF])
    c.w_ff2 = din("w_ff2", [DEPTH, DFF, D])
    c.g_final = din("g_final", [D])
    k_ident = din("k_ident", [128, 128])
    k_bigident = din("k_bigident", [128, 128])
    k_maskA = din("k_maskA", [128, 128])
    k_mbdiag = din("k_mbdiag", [128, 128])
    k_initdiag = din("k_initdiag", [128, 1024])
    k_pow2 = din("k_pow2", [128, KBIS])
    c.k_selw = din("k_selw", [40, 2, 128])
    c.c_C64 = din("k_C64", [128, S])
    c.c_S64 = din("k_S64", [128, S])
    c.c_C32 = din("k_C32", [128, S])
    c.c_S32 = din("k_S32", [128, S])
    c.out = nc.dram_tensor("out", [S, D], F32, kind="ExternalOutput").ap()

    dump = DBG["dump"]

    def scratch(name, shape, dt):
        return nc.dram_tensor(name, shape, dt, kind=("ExternalOutput" if dump else "Internal")).ap()

    c.xs = scratch("xs", [S, D], F32)
    c.qaT = scratch("qaT", [512, S], BF16)
    c.kaT = scratch("kaT", [512, S], BF16)
    c.va = scratch("va", [S, 512], BF16)
    c.qbT = scratch("qbT", [512, S], BF16)
    c.kbT = scratch("kbT", [128, S], BF16)
    c.vb = scratch("vb", [S, 128], BF16)
    c.qiT = scratch("qiT", [256, S], F32)
    c.kiT = scratch("kiT", [32, S], F32)
    c.wT = scratch("wT", [8, S], F32)
    c.qsS = scratch("qsS", [16, 64, S], BF16)
    c.kiS = scratch("kiS", [64, S], BF16)
    c.gates = scratch("gates", [S, 2048], BF16)
    c.ya = scratch("ya", [S, 512], BF16)
    c.yb = scratch("yb", [S, 512], BF16)

    with ExitStack() as es:
        def sbg(name, shape, dt):
            return es.enter_context(nc.sbuf_tensor(name, shape, dt))

        c.ident = sbg("ident", [128, 128], BF16)
        c.bigident = sbg("bigident", [128, 128], BF16)
        c.maskA = sbg("maskA", [128, 128], BF16)
        c.mbdiag = sbg("mbdiag", [128, 128], BF16)
        c.initdiag = sbg("initdiag", [128, 1024], F32)
        c.zeros = sbg("zeros", [128, 1024], F32)
        c.pow2 = sbg("pow2", [128, KBIS], F32)
        eps_t = sbg("eps", [128, 1], F32)
        c.eps_ap = eps_t[:, 0:1]
        c.ps = es.enter_context(nc.psum_tensor("ps", [128, 4096], F32))
        block = es.enter_context(nc.Block())
        sch = Sched(nc, block)
        c.PB = sch.bufs_n(8, "psb")
        c.b_const = sch.buf("const")
        for (dst, src) in ((c.ident, k_ident), (c.bigident, k_bigident), (c.maskA, k_maskA), (c.mbdiag, k_mbdiag)):
            op_dma(sch, "pool", dst[:], src, [], [c.b_const])
        op_dma(sch, "sp", c.initdiag[:], k_initdiag, [], [c.b_const])
        op_dma(sch, "sp", c.pow2[:], k_pow2, [], [c.b_const])
        op_memset(sch, "dve", c.zeros[:], 0.0, [c.b_const])
        op_memset(sch, "dve", eps_t[:], EPS, [c.b_const])
        sch.flush()
        stop = DBG["stop_after"]
        done = False
        for l in range(depth):
            xsrc = c.x if l == 0 else c.xs
            for (nm, fn) in (("p1", lambda: phase1(nc, sch, c, l, xsrc)),
                             ("p2", lambda: phase2(nc, sch, c, l)),
                             ("p3", lambda: phase3(nc, sch, c, l))):
                fn()
                if stop == (l, nm):
                    done = True
                    break
            if done:
                break
            with nc.sbuf_tensor(f"W2pre_{l}", [128, 32, D], BF16) as W2pre:
                phase4a(nc, sch, c, l, xsrc, W2pre)
                if stop == (l, "p4a"):
                    done = True
                else:
                    phase4b(nc, sch, c, l, l == depth - 1, W2pre)
                    if stop == (l, "p4b"):
                        done = True
            if done:
                break
    return nc


def make_consts():
    k = {}
    k["k_ident"] = np.eye(128, dtype=np.float32)
    k["k_bigident"] = (30000.0 * np.eye(128)).astype(np.float32)
    p = np.arange(128)[:, None]
    r = np.arange(128)[None, :]
    k["k_maskA"] = np.where(r < p, 0.0, -30000.0).astype(np.float32)
    adm = (p >= 64) | (r < 64)
    k["k_mbdiag"] = np.where(adm, 1.0, 0.0).astype(np.float32)
    init = np.zeros((128, 1024), np.float32)
    init[:, 896:] = np.where(adm, 0.0, -1e30)
    k["k_initdiag"] = init
    sel = np.zeros((40, 2, 128), np.float32)
    for cc in range(2):
        for hh in range(4):
            sel[32 + 4 * cc + hh, cc, hh * 32:(hh + 1) * 32] = 1.0
    k["k_selw"] = sel
    k["k_pow2"] = np.tile((2.0 ** -(np.arange(KBIS) + 1.0))[None, :], (128, 1)).astype(np.float32)
    t = np.arange(S, dtype=np.float32)

    def tables(rot, hd):
        inv = (np.float32(THETA) ** (-(np.arange(0, rot, 2, dtype=np.float32) / np.float32(rot)))).astype(np.float32)
        ang = (t[:, None] * inv[None, :]).astype(np.float32)
        cs, sn = np.cos(ang).astype(np.float32), np.sin(ang).astype(np.float32)
        C = np.ones((128, S), np.float32)
        Sn = np.zeros((128, S), np.float32)
        for pp in range(128):
            i = pp % hd
            if i < rot:
                C[pp] = cs[:, i % (rot // 2)]
                Sn[pp] = sn[:, i % (rot // 2)]
        return C, Sn

    k["k_C64"], k["k_S64"] = tables(16, 64)
    k["k_C32"], k["k_S32"] = tables(8, 32)
    return k


def kernel(x, g_mix, w_in, w_up_a, w_up_b, w_o, g_mlp, w_ff1, w_ff2, g_final):
    f = lambda a: np.ascontiguousarray(np.asarray(a, dtype=np.float32))
    x = f(x)
    shared = dict(g_mix=f(g_mix), w_in=f(w_in), w_up_a=f(w_up_a), w_up_b=f(w_up_b), w_o=f(w_o), g_mlp=f(g_mlp),
                  w_ff1=f(w_ff1), w_ff2=f(w_ff2), g_final=f(g_final))
    shared.update(make_consts())
    nc = build_nc()
    in_maps = [dict(shared, x=x[b]) for b in range(8)]
    res = run_bass_kernel_spmd(nc, in_maps, core_ids=list(range(8)))
    return np.stack([np.asarray(res.results[b]["out"], dtype=np.float32) for b in range(8)], axis=0)
```

```python
import math
from contextlib import ExitStack

import numpy as np
import concourse.bass as bass
import concourse.mybir as mybir
from concourse.bass_utils import run_bass_kernel_spmd

F32 = mybir.dt.float32
BF16 = mybir.dt.bfloat16
AF = mybir.ActivationFunctionType
ALU = mybir.AluOpType
AX = mybir.AxisListType

S = 4096
D = 1024
NT = S // 128
DEPTH = 2
N_IN = 4648
DFF = 4096
EPS = 1e-6
THETA = 500000.0
KBIS = 13
NKEEP = 256

C_QA, C_KA, C_VA, C_QB, C_KB, C_VB, C_QI, C_KI, C_WI, C_G = 0, 512, 1024, 1536, 2048, 2176, 2304, 2560, 2592, 2600
X_QB, X_KB, X_QI, X_KI = 4648, 4648 + 512, 4648 + 640, 4648 + 896
NX = 4648 + 928

DBG = {"stop_after": None, "dump": False}


class Buf:
    __slots__ = ("name", "writer", "rd_eng", "rd_dma")

    def __init__(self, name=""):
        self.name = name
        self.writer = None
        self.rd_eng = {}
        self.rd_dma = []

    def reset(self):
        self.writer = None
        self.rd_eng = {}
        self.rd_dma = []


class Op:
    __slots__ = ("eng", "fn", "deps", "is_dma", "need_inc", "sem", "val", "seq")


ENGS = ["pe", "act", "dve", "pool", "sp"]
NDMA = 24
NDMA_HW = 16


class Sched:
    def __init__(self, nc, block):
        self.nc = nc
        self.block = block
        self.h = {"pe": nc.tensor, "act": nc.scalar, "dve": nc.vector, "pool": nc.gpsimd, "sp": nc.sync}
        self.bstart = {"pe": block.tensor, "act": block.scalar, "dve": block.vector, "pool": block.gpsimd,
                       "sp": block.sync}
        self.psem = {e: nc.alloc_semaphore(f"prog_{e}") for e in ENGS}
        self.pcnt = {e: 0 for e in ENGS}
        self.dsem = [nc.alloc_semaphore(f"dma_{k}") for k in range(NDMA)]
        self.dval = [0] * NDMA
        self.dlast = [None] * NDMA
        self.drr = 0
        self.drr_sw = 0
        self.ops = {e: [] for e in ENGS}
        self.waited = {e: {} for e in ENGS}
        self.bufs = []
        self.seq = 0
        self.last = {e: None for e in ENGS}
        self.phase_dma = []

    def buf(self, name=""):
        b = Buf(name)
        self.bufs.append(b)
        return b

    def bufs_n(self, n, name=""):
        return [self.buf(f"{name}{i}") for i in range(n)]

    def op(self, eng, fn, reads=(), writes=(), dma=False):
        o = Op()
        o.eng, o.fn, o.is_dma, o.need_inc, o.sem, o.val = eng, fn, dma, dma, None, 0
        o.seq = self.seq
        self.seq += 1
        deps = set()
        for b in reads:
            if b.writer is not None:
                deps.add(b.writer)
        for b in writes:
            if b.writer is not None:
                deps.add(b.writer)
            deps.update(b.rd_eng.values())
            deps.update(b.rd_dma)
        if dma:
            if eng == "pool":
                k = NDMA_HW + self.drr_sw
                self.drr_sw = (self.drr_sw + 1) % (NDMA - NDMA_HW)
            else:
                k = self.drr
                self.drr = (self.drr + 1) % NDMA_HW
            if self.dlast[k] is not None:
                deps.add(self.dlast[k])
            self.dval[k] += 16
            o.sem, o.val = self.dsem[k], self.dval[k]
            self.dlast[k] = o
            self.phase_dma.append(o)
        deps.discard(o)
        o.deps = deps
        for b in reads:
            if dma:
                b.rd_dma.append(o)
            else:
                b.rd_eng[eng] = o
        for b in writes:
            b.writer = o
            b.rd_eng = {}
            b.rd_dma = []
        self.ops[eng].append(o)
        if not dma:
            self.last[eng] = o
        return o

    def barrier(self):
        deps = set(x for x in self.last.values() if x is not None)
        deps.update(x for x in self.dlast if x is not None)
        for e in ENGS:
            o = Op()
            o.eng, o.fn, o.is_dma, o.need_inc, o.sem, o.val = e, None, False, False, None, 0
            o.seq = self.seq
            self.seq += 1
            o.deps = set(deps)
            self.ops[e].append(o)

    def flush(self):
        self.barrier()
        for e in ENGS:
            for o in self.ops[e]:
                for d in o.deps:
                    if d.is_dma:
                        continue
                    if d.eng == "pe" and o.eng == "pe" and not o.is_dma:
                        continue
                    d.need_inc = True
        for e in ENGS:
            for o in self.ops[e]:
                if (not o.is_dma) and o.need_inc:
                    self.pcnt[e] += 1
                    o.sem, o.val = self.psem[e], self.pcnt[e]
        for e in ENGS:
            ops = self.ops[e]
            if not ops:
                continue
            waited = self.waited[e]

            def body(eng, ops=ops, waited=waited, e=e):
                for o in ops:
                    need = {}
                    for d in o.deps:
                        if (not d.is_dma) and d.eng == "pe" and e == "pe" and not o.is_dma:
                            continue
                        s = d.sem
                        key = id(s)
                        if key not in need or need[key][1] < d.val:
                            need[key] = (s, d.val)
                    for key, (s, v) in need.items():
                        if waited.get(key, 0) >= v:
                            continue
                        eng.wait_ge(s, v)
                        waited[key] = v
                    if o.fn is None:
                        continue
                    ins = o.fn(eng)
                    if o.is_dma:
                        ins.then_inc(o.sem, 16)
                    elif o.need_inc:
                        ins.then_inc(o.sem, 1)

            self.bstart[e](body)
        self.ops = {e: [] for e in ENGS}
        for b in self.bufs:
            b.reset()
        self.dlast = [None] * NDMA
        self.last = {e: None for e in ENGS}
        self.phase_dma = []


def op_mm(sch, out, lhsT, rhs, start, stop, reads, writes):
    return sch.op("pe", lambda e: e.matmul(out, lhsT=lhsT, rhs=rhs, start=start, stop=stop), reads, writes)


def op_tr(sch, out, in_, ident, reads, writes):
    return sch.op("pe", lambda e: e.transpose(out, in_, ident), reads, writes)


def op_dma(sch, eng, out, in_, reads, writes, **kw):
    return sch.op(eng, lambda e: e.dma_start(out=out, in_=in_, **kw), reads, writes, dma=True)


def op_act(sch, out, in_, func, reads, writes, **kw):
    return sch.op("act", lambda e: e.activation(out=out, in_=in_, func=func, **kw), reads, writes)


def op_ts(sch, eng, out, in0, s1, s2, op0, op1, reads, writes, accum_out=None):
    if op1 is None:
        return sch.op(eng, lambda e: e.tensor_scalar(out=out, in0=in0, scalar1=s1, scalar2=None, op0=op0), reads,
                      writes)
    if accum_out is None:
        return sch.op(eng, lambda e: e.tensor_scalar(out=out, in0=in0, scalar1=s1, scalar2=s2, op0=op0, op1=op1),
                      reads, writes)
    return sch.op(eng, lambda e: e.tensor_scalar(out=out, in0=in0, scalar1=s1, scalar2=s2, op0=op0, op1=op1,
                                                  accum_out=accum_out), reads, writes)


def op_tt(sch, eng, out, in0, in1, op, reads, writes):
    return sch.op(eng, lambda e: e.tensor_tensor(out=out, in0=in0, in1=in1, op=op), reads, writes)


def op_stt(sch, out, in0, scalar, in1, op0, op1, reads, writes, accum_out=None):
    if accum_out is None:
        return sch.op("dve", lambda e: e.scalar_tensor_tensor(out=out, in0=in0, scalar=scalar, in1=in1, op0=op0,
                                                              op1=op1), reads, writes)
    return sch.op("dve", lambda e: e.scalar_tensor_tensor(out=out, in0=in0, scalar=scalar, in1=in1, op0=op0,
                                                          op1=op1, accum_out=accum_out), reads, writes)


def op_copy(sch, eng, out, in_, reads, writes):
    if eng == "act":
        return sch.op("act", lambda e: e.copy(out=out, in_=in_), reads, writes)
    return sch.op(eng, lambda e: e.tensor_copy(out=out, in_=in_), reads, writes)


def op_memset(sch, eng, ap, val, writes):
    return sch.op(eng, lambda e: e.memset(ap, val), (), writes)


class Ctx:
    pass


def rms_rstd(sch, c, xt_ap, b_x, junk_ap, b_junk, sc_ap, b_sc):
    op_stt(sch, junk_ap, xt_ap, 1.0, xt_ap, ALU.mult, ALU.mult, [b_x], [b_junk, b_sc], accum_out=sc_ap[:, 0:1])
    op_act(sch, sc_ap[:, 1:2], sc_ap[:, 0:1], AF.Sqrt, [b_sc], [b_sc], scale=1.0 / D, bias=c.eps_ap)
    sch.op("dve", lambda e: e.reciprocal(sc_ap[:, 2:3], sc_ap[:, 1:2]), [b_sc], [b_sc])


def load_bcast_vec(sch, dst_ap, b_dst, vec_ap):
    op_dma(sch, "sp", dst_ap, vec_ap.partition_broadcast(128), [], [b_dst])


def phase1(nc, sch, c, l, xsrc):
    with ExitStack() as es:
        def sb(name, shape, dt):
            return es.enter_context(nc.sbuf_tensor(f"p1_{l}_{name}", shape, dt))

        Wx = sb("Wx", [128, 8, NX], BF16)
        W_EDGES = (0, 512, 1024, 1536, 2048, 4096, N_IN, NX)
        b_Wr = sch.bufs_n(len(W_EDGES) - 1)
        R_PRIME = len(W_EDGES) - 2

        def wb(col, n):
            return [b_Wr[r] for r in range(len(W_EDGES) - 1) if col < W_EDGES[r + 1] and col + n > W_EDGES[r]]
        gb = sb("gb", [128, D], F32)
        b_gb = sch.buf()
        xt = [sb(f"xt{i}", [128, D], F32) for i in range(2)]
        b_xt = sch.bufs_n(2)
        junk = sb("junk", [128, D], BF16)
        b_junk = sch.buf()
        scr = [sb(f"scr{i}", [128, 4], F32) for i in range(2)]
        b_scr = sch.bufs_n(2)
        hb = [sb(f"hb{i}", [128, D], BF16) for i in range(2)]
        b_hb = sch.bufs_n(2)
        hT = [sb(f"hT{i}", [128, 8, 512], BF16) for i in range(2)]
        b_hT = sch.bufs_n(2)
        tab = [[sb(f"tab{i}_{j}", [128, 512], F32) for j in range(4)] for i in range(2)]
        b_tab = [sch.bufs_n(4) for _ in range(2)]
        fmo = [sb(f"fmo{i}", [128, 512], BF16) for i in range(3)]
        b_fmo = sch.bufs_n(3)
        fmf = [sb(f"fmf{i}", [128, 512], F32) for i in range(2)]
        b_fmf = sch.bufs_n(2)
        tmp = [sb(f"tmp{i}", [128, 512], F32) for i in range(4)]
        b_tmp = sch.bufs_n(4)
        tvo = [sb(f"tvo{i}", [128, 512], BF16) for i in range(2)]
        b_tvo = sch.bufs_n(2)
        gto = [sb(f"gto{i}", [128, 2048], BF16) for i in range(2)]
        b_gto = sch.bufs_n(2)
        kw = [sb(f"kw{i}", [64, 512], F32) for i in range(2)]
        b_kw = sch.bufs_n(2)
        selw = sb("selw", [40, 2, 128], F32)
        b_selw = sch.buf()
        tq = [sb(f"tq{i}", [128, 512], F32) for i in range(2)]
        h32 = [sb(f"h32{i}", [128, 512], F32) for i in range(2)]
        hib = [sb(f"hib{i}", [128, 512], BF16) for i in range(2)]
        lob = [sb(f"lob{i}", [128, 512], BF16) for i in range(2)]
        b_tq, b_h32, b_hib, b_lob = sch.bufs_n(2), sch.bufs_n(2), sch.bufs_n(2), sch.bufs_n(2)
        op_dma(sch, "sp", selw[:], c.k_selw, [], [b_selw])
        rnd = [0]

        def split_store(src_ap, b_src, npart, stores):
            r = rnd[0] % 2
            rnd[0] += 1
            op_copy(sch, "act", hib[r][0:npart, :], src_ap, [b_src], [b_hib[r]])
            op_copy(sch, "pool", h32[r][0:npart, :], hib[r][0:npart, :], [b_hib[r]], [b_h32[r]])
            op_tt(sch, "dve", lob[r][0:npart, :], src_ap, h32[r][0:npart, :], ALU.subtract, [b_src, b_h32[r]],
                  [b_lob[r]])
            for (d_hi, d_lo, r0, r1) in stores:
                op_dma(sch, "sp", d_hi, hib[r][r0:r1, :], [b_hib[r]], [])
                op_dma(sch, "sp", d_lo, lob[r][r0:r1, :], [b_lob[r]], [])

        wsrc = c.w_in[l].rearrange("(k p) n -> p k n", p=128)
        for r_ in range(R_PRIME):
            a, b = W_EDGES[r_], W_EDGES[r_ + 1]
            if b - a <= 512:
                op_dma(sch, "pool", Wx[:, :, a:b], wsrc[:, :, a:b], [], [b_Wr[r_]])
            else:
                for kc in range(8):
                    op_dma(sch, "pool", Wx[:, kc, a:b], wsrc[:, kc, a:b], [], [b_Wr[r_]])
        load_bcast_vec(sch, gb[:], b_gb, c.g_mix[l])
        for (src0, dst0, nh, hd, half) in ((C_QB, X_QB, 8, 64, 8), (C_KB, X_KB, 2, 64, 8), (C_QI, X_QI, 8, 32, 4),
                                           (C_KI, X_KI, 1, 32, 4)):
            w = nh * hd
            op_memset(sch, "dve", Wx[:, :, dst0:dst0 + w], 0.0, [b_Wr[R_PRIME]])
            for kc in range(8):
                sv = Wx[:, kc, src0:src0 + w].rearrange("p (h d) -> p h d", h=nh)
                dv = Wx[:, kc, dst0:dst0 + w].rearrange("p (h d) -> p h d", h=nh)
                op_ts(sch, "dve", dv[:, :, 0:half], sv[:, :, half:2 * half], -1.0, None, ALU.mult, None,
                      wb(src0, w), [b_Wr[R_PRIME]])
                op_copy(sch, "dve", dv[:, :, half:2 * half], sv[:, :, 0:half], wb(src0, w), [b_Wr[R_PRIME]])
        op_ts(sch, "dve", Wx[:, :, C_WI:C_WI + 8], Wx[:, :, C_WI:C_WI + 8], 1.0 / 16.0, None, ALU.mult, None,
              wb(C_WI, 8), wb(C_WI, 8))

        PB = c.PB
        ps = c.ps
        fm_cnt = [0]
        tm_cnt = [0]
        tile_cnt = [0]

        def fm_chunk(g, hTg, b_hTg, wcol, m, pcol, pm, tabk, dst, fp32out, tabs, b_tabs, obuf=None):
            k = fm_cnt[0]
            fm_cnt[0] += 1
            bank_m = 1 + (k % 2)
            bank_p = 3 + (k % 2)
            pm_ap = ps[0:m, bank_m * 512:(bank_m + 1) * 512]
            for kc in range(8):
                op_mm(sch, pm_ap, Wx[:, kc, wcol:wcol + m], hTg[:, kc, :], kc == 0, kc == 7, wb(wcol, m) + [b_hTg],
                      [PB[bank_m]])
            tsl = slice(g * 512, (g + 1) * 512)
            if pcol is None:
                o = fmo[k % 3]
                bo = b_fmo[k % 3]
                op_copy(sch, "act", o[0:m, :], pm_ap, [PB[bank_m]], [bo])
                for (d_ap, r0, r1) in dst:
                    op_dma(sch, "sp", d_ap[:, tsl], o[r0:r1, :], [bo], [])
                return
            pp_ap = ps[0:pm, bank_p * 512:(bank_p + 1) * 512]
            for kc in range(8):
                op_mm(sch, pp_ap, Wx[:, kc, pcol:pcol + pm], hTg[:, kc, :], kc == 0, kc == 7,
                      wb(pcol, pm) + [b_hTg], [PB[bank_p]])
            t1 = tmp[(2 * k) % 4]
            bt1 = b_tmp[(2 * k) % 4]
            t2 = tmp[(2 * k + 1) % 4]
            bt2 = b_tmp[(2 * k + 1) % 4]
            Ct, St = tabs[2 * tabk], tabs[2 * tabk + 1]
            bC, bS = b_tabs[2 * tabk], b_tabs[2 * tabk + 1]
            op_tt(sch, "dve", t1[0:pm, :], pm_ap[0:pm, :], Ct[0:pm, :], ALU.mult, [PB[bank_m], bC], [bt1])
            op_tt(sch, "dve", t2[0:pm, :], pp_ap, St[0:pm, :], ALU.mult, [PB[bank_p], bS], [bt2])
            if obuf is not None:
                o, bo = obuf
            elif fp32out:
                o = fmf[k % 2]
                bo = b_fmf[k % 2]
            else:
                o = fmo[k % 3]
                bo = b_fmo[k % 3]
            op_tt(sch, "pool", o[0:pm, :], t1[0:pm, :], t2[0:pm, :], ALU.add, [bt1, bt2], [bo])
            if m > pm:
                op_copy(sch, "act", o[pm:m, :], pm_ap[pm:m, :], [PB[bank_m]], [bo])
            for (d_ap, r0, r1) in dst:
                op_dma(sch, "sp", d_ap[:, tsl], o[r0:r1, :], [bo], [])
            return o, bo, k

        for g in range(8):
            hTg = hT[g % 2]
            b_hTg = b_hT[g % 2]
            tabs = tab[g % 2]
            b_tabs = b_tab[g % 2]
            tsl = slice(g * 512, (g + 1) * 512)
            for j, tsrc in enumerate((c.c_C64, c.c_S64, c.c_C32, c.c_S32)):
                op_dma(sch, "sp", tabs[j][:], tsrc[:, tsl], [], [b_tabs[j]])
            for j in range(4):
                t = g * 4 + j
                k = tile_cnt[0]
                tile_cnt[0] += 1
                x_t = xt[k % 2]
                bx = b_xt[k % 2]
                sc = scr[k % 2]
                bsc = b_scr[k % 2]
                h_t = hb[k % 2]
                bh = b_hb[k % 2]
                op_dma(sch, "sp", x_t[:], xsrc[t * 128:(t + 1) * 128, :], [], [bx])
                rms_rstd(sch, c, x_t[:], bx, junk[:], b_junk, sc[:], bsc)
                op_stt(sch, h_t[:], x_t[:], sc[:, 2:3], gb[:], ALU.mult, ALU.mult, [bx, bsc, b_gb], [bh])
                tp = ps[:, 0:512].bitcast(BF16)
                for kc in range(8):
                    op_tr(sch, tp[:, kc * 128:(kc + 1) * 128], h_t[:, kc * 128:(kc + 1) * 128], c.ident[:],
                          [bh, c.b_const], [PB[0]])
                op_copy(sch, "act", hTg[:, :, j * 128:(j + 1) * 128], tp.rearrange("p (k t) -> p k t", k=8),
                        [PB[0]], [b_hTg])
            for cc in range(4):
                fm_chunk(g, hTg, b_hTg, C_QA + 128 * cc, 128, None, 0, 0,
                         [(c.qaT[128 * cc:128 * cc + 128], 0, 128)], False, tabs, b_tabs)
            for cc in range(4):
                fm_chunk(g, hTg, b_hTg, C_KA + 128 * cc, 128, None, 0, 0,
                         [(c.kaT[128 * cc:128 * cc + 128], 0, 128)], False, tabs, b_tabs)
            for cc in range(4):
                fm_chunk(g, hTg, b_hTg, C_QB + 128 * cc, 128, X_QB + 128 * cc, 128, 0,
                         [(c.qbT[128 * cc:128 * cc + 128], 0, 128)], False, tabs, b_tabs)
            fm_chunk(g, hTg, b_hTg, C_KB, 128, X_KB, 128, 0, [(c.kbT, 0, 128)], False, tabs, b_tabs)
            kwg, b_kwg = kw[g % 2], b_kw[g % 2]
            fm_chunk(g, hTg, b_hTg, C_KI, 40, X_KI, 32, 1, [], True, tabs, b_tabs, obuf=(kwg, b_kwg))
            split_store(kwg[0:32, :], b_kwg, 32, [(c.kiS[0:32, tsl], c.kiS[32:64, tsl], 0, 32)])
            for cc in range(2):
                o, bo, kk = fm_chunk(g, hTg, b_hTg, C_QI + 128 * cc, 128, X_QI + 128 * cc, 128, 1, [], True, tabs,
                                     b_tabs)
                kx = fm_cnt[0]
                fm_cnt[0] += 1
                bank_w = 1 + (kx % 2)
                wrep = ps[:, bank_w * 512:(bank_w + 1) * 512]
                op_mm(sch, wrep, selw[:, cc, :], kwg[0:40, :], True, True, [b_selw, b_kwg], [PB[bank_w]])
                for sg, alu in enumerate((ALU.max, ALU.min)):
                    r = rnd[0] % 2
                    op_stt(sch, tq[r][:, :], wrep, 0.0, o[:, :], alu, ALU.mult, [PB[bank_w], bo], [b_tq[r]])
                    stores = []
                    for hh in range(4):
                        a = 2 * (4 * cc + hh) + sg
                        stores.append((c.qsS[a, 0:32, tsl], c.qsS[a, 32:64, tsl], hh * 32, (hh + 1) * 32))
                    split_store(tq[r][:, :], b_tq[r], 128, stores)
            for j in range(4):
                t = g * 4 + j
                rows = slice(t * 128, (t + 1) * 128)
                lhs = [hTg[:, kc, j * 128:(j + 1) * 128] for kc in range(8)]
                gt = gto[t % 2]
                bg = b_gto[t % 2]
                for (wcol, n, kind) in ((C_VA, 512, "va"), (C_VB, 128, "vb"), (C_G, 512, 0), (C_G + 512, 512, 1),
                                        (C_G + 1024, 512, 2), (C_G + 1536, 512, 3)):
                    k = tm_cnt[0]
                    tm_cnt[0] += 1
                    bank = 5 + (k % 3)
                    p_ap = ps[:, bank * 512:bank * 512 + n]
                    for kc in range(8):
                        op_mm(sch, p_ap, lhs[kc], Wx[:, kc, wcol:wcol + n], kc == 0, kc == 7,
                              wb(wcol, n) + [b_hTg], [PB[bank]])
                    if kind == "va" or kind == "vb":
                        o = tvo[k % 2]
                        bo = b_tvo[k % 2]
                        op_copy(sch, "act", o[:, 0:n], p_ap, [PB[bank]], [bo])
                        dstt = c.va if kind == "va" else c.vb
                        op_dma(sch, "sp", dstt[rows, :], o[:, 0:n], [bo], [])
                    else:
                        op_act(sch, gt[:, kind * 512:(kind + 1) * 512], p_ap, AF.Sigmoid, [PB[bank]], [bg])
                op_dma(sch, "sp", c.gates[rows, :], gt[:], [bg], [])
        sch.flush()


def phase2(nc, sch, c, l):
    CW = 1024
    with ExitStack() as es:
        def sb(name, shape, dt):
            return es.enter_context(nc.sbuf_tensor(f"p2_{l}_{name}", shape, dt))

        QT = [sb(f"QT{i}", [64, S], BF16) for i in range(2)]
        KT = [sb(f"KT{i}", [64, S], BF16) for i in range(2)]
        b_QK = sch.bufs_n(2)
        V = sb("V", [128, NT, 512], BF16)
        b_V = sch.buf()
        om = [sb(f"om{i}", [128, S], F32) for i in range(2)]
        b_om = [sch.bufs_n(S // CW) for _ in range(2)]
        P = [sb(f"P{i}", [128, S + 4], F32) for i in range(2)]
        b_P = [sch.bufs_n(S // CW + 1) for _ in range(2)]
        NA = 7
        A = [sb(f"A{i}", [128, CW], BF16) for i in range(NA)]
        b_A = sch.bufs_n(NA)
        AT = [sb(f"AT{i}", [128, CW], BF16) for i in range(3)]
        b_AT = sch.bufs_n(3)
        ya = sb("ya", [128, NT, 512], BF16)
        b_ya = sch.buf()
        PB = c.PB
        ps = c.ps

        vsrc = c.va.rearrange("(n p) d -> p n d", p=128)
        for q in range(4):
            op_dma(sch, "sp", V[:, q * 8:(q + 1) * 8, :], vsrc[:, q * 8:(q + 1) * 8, :], [], [b_V])

        def load_head(h):
            s = h % 2
            op_dma(sch, "sp", QT[s][:], c.qaT[h * 64:(h + 1) * 64, :], [], [b_QK[s]])
            op_dma(sch, "sp", KT[s][:], c.kaT[h * 64:(h + 1) * 64, :], [], [b_QK[s]])

        jobs = []
        for h in range(8):
            for i in range(NT):
                n = (i + 1) * 128
                nch = (n + CW - 1) // CW
                for cc in reversed(range(nch)):
                    jobs.append((h, i, cc, nch))
        LAG = 4
        zc = [0]
        tc_ = [0]

        def stage1(j):
            h, i, cc, nch = jobs[j]
            n = (i + 1) * 128
            c0 = cc * CW
            ln = min(CW, n - c0)
            s = h % 2
            slot = (h * NT + i) % 2
            first = (cc == nch - 1)
            if first and i == 0:
                if h + 1 < 8:
                    load_head(h + 1)
            if first:
                op_memset(sch, "dve", P[slot][:, n:n + 1], 1.0, [b_P[slot][n // CW]])
            zb = 2 * (zc[0] % 2)
            zc[0] += 1
            zw = [PB[zb], PB[zb + 1]]
            q_ap = QT[s][:, i * 128:(i + 1) * 128]
            for sub in range(0, ln, 512):
                sln = min(512, ln - sub)
                lastsub = (sub + sln == ln)
                op_mm(sch, ps[:, zb * 512 + sub:zb * 512 + sub + sln], q_ap, KT[s][:, c0 + sub:c0 + sub + sln], True,
                      not (first and lastsub), [b_QK[s]], zw)
            if first:
                op_mm(sch, ps[:, zb * 512 + ln - 128:zb * 512 + ln], c.ident[:], c.maskA[:], False, True,
                      [c.b_const], zw)
            op_act(sch, om[slot][:, c0:c0 + ln], ps[:, zb * 512:zb * 512 + ln], AF.Sigmoid, zw, [b_om[slot][cc]],
                   scale=-0.125)
            nxt = (c0 + ln) // CW
            sch.op("dve", lambda e: e.tensor_tensor_scan(out=P[slot][:, c0:c0 + ln][:, ::-1],
                                                         data0=om[slot][:, c0:c0 + ln][:, ::-1],
                                                         data1=om[slot][:, c0:c0 + ln][:, ::-1],
                                                         initial=P[slot][:, c0 + ln:c0 + ln + 1],
                                                         op0=ALU.mult, op1=ALU.bypass),
                   [b_om[slot][cc], b_P[slot][nxt]], [b_P[slot][cc]])
            a = A[j % NA]
            op_tt(sch, "pool", a[:, 0:ln], P[slot][:, c0 + 1:c0 + ln + 1], P[slot][:, c0:c0 + ln], ALU.subtract,
                  [b_P[slot][cc], b_P[slot][nxt]], [b_A[j % NA]])

        def stage2(j):
            h, i, cc, nch = jobs[j]
            n = (i + 1) * 128
            c0 = cc * CW
            ln = min(CW, n - c0)
            nb = ln // 128
            a = A[j % NA]
            k = tc_[0]
            tc_[0] += 1
            tbank = (4, 7)[k % 2]
            tp = ps[:, tbank * 512:tbank * 512 + 512].bitcast(BF16)
            for b in range(nb):
                op_tr(sch, tp[:, b * 128:(b + 1) * 128], a[:, b * 128:(b + 1) * 128], c.ident[:],
                      [b_A[j % NA], c.b_const], [PB[tbank]])
            at = AT[k % 3]
            op_copy(sch, "act", at[:, 0:ln], tp[:, 0:ln], [PB[tbank]], [b_AT[k % 3]])
            ybank = 5 + ((h * NT + i) % 2)
            y = ps[:, ybank * 512:ybank * 512 + 64]
            for b in range(nb):
                kb = (c0 // 128) + b
                first = (cc == nch - 1) and b == 0
                last = (cc == 0) and b == nb - 1
                op_mm(sch, y, at[:, b * 128:(b + 1) * 128], V[:, kb, h * 64:(h + 1) * 64], first, last,
                      [b_AT[k % 3], b_V], [PB[ybank]])
            if cc == 0:
                op_copy(sch, "act", ya[:, i, h * 64:(h + 1) * 64], y, [PB[ybank]], [b_ya])

        load_head(0)
        J = len(jobs)
        for j in range(J + LAG):
            if j < J:
                stage1(j)
            if j - LAG >= 0:
                stage2(j - LAG)
        ydst = c.ya.rearrange("(n p) d -> p n d", p=128)
        for q in range(4):
            op_dma(sch, "sp", ydst[:, q * 8:(q + 1) * 8, :], ya[:, q * 8:(q + 1) * 8, :], [b_ya], [])
        sch.flush()


def phase3(nc, sch, c, l):
    with ExitStack() as es:
        def sb(name, shape, dt):
            return es.enter_context(nc.sbuf_tensor(f"p3_{l}_{name}", shape, dt))

        kiR = sb("kiR", [96, S], BF16)
        b_ki = sch.buf()
        qst = [sb(f"qst{i}", [96, 16, 128], BF16) for i in range(2)]
        b_qst = sch.bufs_n(2)
        d_qsS = sch.buf()
        d_kiS = sch.buf()
        qbT = sb("qbT", [128, 4, S], BF16)
        b_qb = sch.buf()
        kbR = sb("kbR", [128, 2, S], BF16)
        b_kb = sch.buf()
        Vb = sb("Vb", [128, NT, 2, 65], BF16)
        b_V = sch.buf()
        ybu = [sb(f"ybu{i}", [128, 8, 65], F32) for i in range(2)]
        b_ybu = sch.bufs_n(2)
        r8 = [sb(f"r8{i}", [128, 8], F32) for i in range(2)]
        m8 = [sb(f"m8{i}", [128, 32, 8], F32) for i in range(2)]
        acc = [sb(f"acc{i}", [128, S], F32) for i in range(2)]
        b_acc = [sch.bufs_n(8) for _ in range(2)]
        mb = [sb(f"mb{i}", [128, S], BF16) for i in range(2)]
        b_mb = sch.bufs_n(2)
        jk = sb("jk", [128, S], BF16)
        b_jk = sch.buf()
        bs = [sb(f"bs{i}", [128, 8], F32) for i in range(2)]
        b_bs = sch.bufs_n(2)
        steps = [sb(f"steps{i}", [128, KBIS], F32) for i in range(2)]
        NE = 9
        E = [sb(f"E{i}", [128, 512], BF16) for i in range(3)]
        b_E = sch.bufs_n(3)
        EM = [sb(f"EM{i}", [128, 512], BF16) for i in range(NE)]
        b_EM = sch.bufs_n(NE)
        MT = [sb(f"MT{i}", [128, S], BF16) for i in range(2)]
        b_MT = sch.bufs_n(2)
        rsp = [sb(f"rsp{i}", [128, 12], F32) for i in range(4)]
        b_rsp = sch.bufs_n(4)
        ybt = [sb(f"ybt{i}", [128, 512], BF16) for i in range(2)]
        b_ybt = sch.bufs_n(2)
        PB = c.PB
        ps = c.ps

        op_dma(sch, "sp", kiR[0:32, :], c.kiS[0:32, :], [], [b_ki])
        op_dma(sch, "sp", kiR[32:64, :], c.kiS[0:32, :], [], [b_ki])
        op_dma(sch, "sp", kiR[64:96, :], c.kiS[32:64, :], [], [b_ki])
        qsrc = c.qbT.rearrange("(k p) t -> p k t", p=128)
        for k in range(4):
            op_dma(sch, "sp", qbT[:, k, :], qsrc[:, k, :], [], [b_qb])
        for g in range(2):
            for r in range(2):
                op_dma(sch, "sp", kbR[r * 64:(r + 1) * 64, g, :], c.kbT[g * 64:(g + 1) * 64, :], [], [b_kb])
        vsrc = c.vb.rearrange("(n p) d -> p n d", p=128)
        op_memset(sch, "pool", Vb[:, :, :, 64:65], 1.0, [b_V])
        for q in range(4):
            for g in range(2):
                op_dma(sch, "sp", Vb[:, q * 8:(q + 1) * 8, g, 0:64], vsrc[:, q * 8:(q + 1) * 8, g * 64:(g + 1) * 64],
                       [], [b_V])
        sc_cnt = [0]

        def score_units(i):
            units = []
            n = (i + 1) * 128
            nch2 = (n + 1023) // 1024
            sl = i % 2

            def load():
                op_dma(sch, "sp", qst[sl][0:64, :, :],
                       c.qsS[:, :, i * 128:(i + 1) * 128].rearrange("a r t -> r a t"), [d_qsS], [b_qst[sl]])
                op_dma(sch, "sp", qst[sl][64:96, :, :],
                       c.qsS[:, 0:32, i * 128:(i + 1) * 128].rearrange("a r t -> r a t"), [d_qsS], [b_qst[sl]])

            def pair(cc, a_idx, first):
                c0 = cc * 1024
                ln = min(1024, n - c0)
                a_ap = acc[sl][:, c0:c0 + ln]
                ab = b_acc[sl][2 * cc:2 * cc + (2 if ln > 512 else 1)]
                diag = (cc == nch2 - 1)
                alu = ALU.max if (a_idx % 2 == 0) else ALU.min
                k = sc_cnt[0]
                sc_cnt[0] += 1
                zb = (2, 6)[k % 2]
                zw = [PB[zb], PB[zb + 1]]
                for sub in range(0, ln, 512):
                    sln = min(512, ln - sub)
                    op_mm(sch, ps[:, zb * 512 + sub:zb * 512 + sub + sln], qst[sl][:, a_idx, :],
                          kiR[:, c0 + sub:c0 + sub + sln], True, True, [b_qst[sl], b_ki], zw)
                y = ps[:, zb * 512:zb * 512 + ln]
                if first:
                    in1 = c.initdiag[:, 1024 - ln:1024] if diag else c.zeros[:, 0:ln]
                    op_stt(sch, a_ap, y, 0.0, in1, alu, ALU.add, zw + [c.b_const], ab)
                else:
                    op_stt(sch, a_ap, y, 0.0, a_ap, alu, ALU.add, zw + ab, ab)

            for cc in range(nch2):
                for a_idx in range(16):
                    if cc == 0 and a_idx == 0:
                        units.append(lambda: (load(), pair(0, 0, True)))
                    else:
                        units.append((lambda cc, a_idx: (lambda: pair(cc, a_idx, a_idx == 0)))(cc, a_idx))
            return units

        mt_cnt = [0]

        def mask_T(i):
            n = (i + 1) * 128
            sl = i % 2
            for c0 in range(0, n, 512):
                ln = min(512, n - c0)
                k = mt_cnt[0]
                mt_cnt[0] += 1
                tbank = 2
                tp = ps[:, tbank * 512:tbank * 512 + 512].bitcast(BF16)
                for b in range(ln // 128):
                    op_tr(sch, tp[:, b * 128:(b + 1) * 128], mb[sl][:, c0 + b * 128:c0 + (b + 1) * 128], c.ident[:],
                          [b_mb[sl], c.b_const], [PB[tbank]])
                op_copy(sch, "act", MT[sl][:, c0:c0 + ln], tp[:, 0:ln], [PB[tbank]], [b_MT[sl]])

        def thresh_steps(i):
            n = (i + 1) * 128
            nch = (n + 511) // 512
            sl = i % 2
            m = mb[sl]
            if i < 2:
                def const_mask():
                    if n > 128:
                        op_memset(sch, "pool", m[:, 0:n - 128], 1.0, [b_mb[sl]])
                    op_copy(sch, "pool", m[:, n - 128:n], c.mbdiag[:], [c.b_const], [b_mb[sl]])
                    mask_T(i)
                return [const_mask]
            a = acc[sl]
            ba = b_acc[sl][0:nch]
            b = bs[sl]
            bb = b_bs[sl]
            st = steps[sl]
            mm8 = m8[sl]
            seg = (n - 128) // 32

            def init():
                for jj in range(32):
                    sch.op("dve", (lambda jj: (lambda e: e.max(out=mm8[:, jj, :],
                                                               in_=a[:, jj * seg:(jj + 1) * seg])))(jj), ba, [bb])
                sch.op("dve", lambda e: e.tensor_reduce(out=b[:, 0:1], in_=mm8[:, :, 7], axis=AX.X, op=ALU.min), [bb],
                       [bb])
                sch.op("dve", lambda e: e.tensor_reduce(out=b[:, 6:7], in_=mm8[:, :, 7], axis=AX.X, op=ALU.max), [bb],
                       [bb])
                sch.op("dve", lambda e: e.tensor_reduce(out=b[:, 7:8], in_=a[:, n - 128:n], axis=AX.X, op=ALU.max),
                       ba, [bb])
                op_tt(sch, "dve", b[:, 1:2], b[:, 6:7], b[:, 7:8], ALU.max, [bb], [bb])
                op_tt(sch, "dve", b[:, 2:3], b[:, 1:2], b[:, 0:1], ALU.subtract, [bb], [bb])
                op_ts(sch, "dve", st[:, :], c.pow2[:, :], b[:, 2:3], None, ALU.mult, None, [bb, c.b_const], [bb])
                op_ts(sch, "dve", b[:, 3:4], b[:, 0:1], st[:, 0:1], -1.0, ALU.add, ALU.mult, [bb], [bb])

            def it_a(k):
                op_act(sch, jk[:, 0:n], a[:, 0:n], AF.Sign, ba + [bb], [b_jk, bb], bias=b[:, 3:4], scale=1.0,
                       accum_out=b[:, 4:5])

            def it_b(k):
                op_ts(sch, "dve", b[:, 5:6], b[:, 4:5], 511.0 - n, st[:, k:k + 1], ALU.is_ge, ALU.mult, [bb], [bb])
                op_tt(sch, "dve", b[:, 0:1], b[:, 0:1], b[:, 5:6], ALU.add, [bb], [bb])
                if k + 1 < KBIS:
                    op_ts(sch, "dve", b[:, 3:4], b[:, 0:1], st[:, k + 1:k + 2], -1.0, ALU.add, ALU.mult, [bb], [bb])

            def fin():
                op_ts(sch, "dve", m[:, 0:n], a[:, 0:n], b[:, 0:1], None, ALU.is_ge, None, ba + [bb], [b_mb[sl]])
                mask_T(i)

            steps_ = [init]
            for k in range(KBIS):
                steps_.append((lambda k: (lambda: it_a(k)))(k))
                steps_.append((lambda k: (lambda: it_b(k)))(k))
            return steps_ + [fin]

        jobs = []
        order = list(range(NT - 1, -1, -1))
        for i in order:
            n = (i + 1) * 128
            nch = (n + 511) // 512
            for h in range(8):
                for cc in range(nch):
                    jobs.append((i, h, cc, nch))
        LAG = 7
        zc = [0]
        tc_ = [0]

        def stage1(j):
            i, h, cc, nch = jobs[j]
            n = (i + 1) * 128
            c0 = cc * 512
            ln = min(512, n - c0)
            nb = ln // 128
            sl = i % 2
            g = h // 4
            pb = (h % 2) * 64
            bank = zc[0] % 2
            zc[0] += 1
            q_ap = qbT[pb:pb + 64, h // 2, i * 128:(i + 1) * 128]
            for b in range(nb):
                op_mm(sch, ps[:, bank * 512 + b * 128:bank * 512 + (b + 1) * 128],
                      kbR[pb:pb + 64, g, c0 + b * 128:c0 + (b + 1) * 128], q_ap, True, True, [b_qb, b_kb],
                      [PB[bank]])
            e_ = E[j % 3]
            op_act(sch, e_[:, 0:ln], ps[:, bank * 512:bank * 512 + ln], AF.Exp, [PB[bank]], [b_E[j % 3]], scale=0.125)
            op_tt(sch, "pool", EM[j % NE][:, 0:ln], e_[:, 0:ln], MT[sl][:, c0:c0 + ln], ALU.mult,
                  [b_E[j % 3], b_MT[sl]], [b_EM[j % NE]])

        def stage2(j):
            i, h, cc, nch = jobs[j]
            n = (i + 1) * 128
            c0 = cc * 512
            ln = min(512, n - c0)
            nb = ln // 128
            g = h // 4
            em = EM[j % NE]
            ybank = 4 + ((i * 8 + h) % 2)
            y = ps[:, ybank * 512:ybank * 512 + 65]
            for b in range(nb):
                kb = (c0 // 128) + b
                op_mm(sch, y, em[:, b * 128:(b + 1) * 128], Vb[:, kb, g, :], cc == 0 and b == 0,
                      cc == nch - 1 and b == nb - 1, [b_EM[j % NE], b_V], [PB[ybank]])
            if cc == nch - 1:
                yu = ybu[i % 2]
                byu = b_ybu[i % 2]
                op_copy(sch, "act", yu[:, h, :], y, [PB[ybank]], [byu])
                if h == 7:
                    rr8 = r8[i % 2]
                    yt = ybt[i % 2]
                    sch.op("dve", lambda e: e.reciprocal(rr8[:, :], yu[:, :, 64]), [byu], [byu])
                    for hh in range(8):
                        op_ts(sch, "dve", yt[:, hh * 64:(hh + 1) * 64], yu[:, hh, 0:64], rr8[:, hh:hh + 1], None,
                              ALU.mult, None, [byu], [b_ybt[i % 2]])
                    op_dma(sch, "sp", c.yb[i * 128:(i + 1) * 128, :], yt[:], [b_ybt[i % 2]], [])

        starts = {}
        ends = {}
        for j, (i, h, cc, nch) in enumerate(jobs):
            starts.setdefault(i, j)
            ends[i] = j + 1
        J = len(jobs)

        t0, t1 = order[0], order[1]
        for u in score_units(t0):
            u()
        thr0 = thresh_steps(t0)
        scu1 = score_units(t1)
        si = 0
        for ti, u in enumerate(thr0):
            u()
            s_end = min(len(scu1), ((ti + 1) * len(scu1)) // len(thr0))
            while si < s_end:
                scu1[si]()
                si += 1
        while si < len(scu1):
            scu1[si]()
            si += 1
        for p, i in enumerate(order):
            thr = thresh_steps(order[p + 1]) if p + 1 < NT else []
            scu = []
            if p + 2 < NT and order[p + 2] >= 2:
                scu = score_units(order[p + 2])
            nj = ends[i] - starts[i]
            ti = si = 0
            for jj, j in enumerate(range(starts[i], ends[i])):
                stage1(j)
                if j - LAG >= 0:
                    stage2(j - LAG)
                t_end = min(len(thr), ((jj + 1) * len(thr)) // max(1, int(0.8 * nj)))
                s_end = min(len(scu), ((jj + 1) * len(scu)) // max(1, int(0.9 * nj)))
                while si < s_end or ti < t_end:
                    if si < s_end:
                        scu[si]()
                        si += 1
                    if ti < t_end:
                        thr[ti]()
                        ti += 1
        for j in range(J - LAG, J):
            stage2(j)
        sch.flush()


def phase4a(nc, sch, c, l, xsrc, W2pre=None):
    with ExitStack() as es:
        def sb(name, shape, dt):
            return es.enter_context(nc.sbuf_tensor(f"p4a_{l}_{name}", shape, dt))

        Wua = sb("Wua", [128, 4, D], BF16)
        Wub = sb("Wub", [128, 4, D], BF16)
        Wo = sb("Wo", [128, 8, D], BF16)
        b_Wu = sch.buf()
        b_Wo = sch.buf()
        yt = [sb(f"yt{i}", [128, 1024], BF16) for i in range(3)]
        b_yt = sch.bufs_n(3)
        gt = [sb(f"gt{i}", [128, 2048], BF16) for i in range(3)]
        b_gt = sch.bufs_n(3)
        xt = [sb(f"xt{i}", [128, D], F32) for i in range(3)]
        b_xt = sch.bufs_n(3)
        yT = [sb(f"yT{i}", [128, 8, 128], BF16) for i in range(2)]
        b_yT = sch.bufs_n(2)
        m1 = [sb(f"m1_{i}", [128, D], F32) for i in range(2)]
        m2 = [sb(f"m2_{i}", [128, D], F32) for i in range(2)]
        b_m1 = sch.bufs_n(2)
        b_m2 = sch.bufs_n(2)
        mg = [sb(f"mg{i}", [128, D], BF16) for i in range(2)]
        b_mg = sch.bufs_n(2)
        mT = [sb(f"mT{i}", [128, 8, 128], BF16) for i in range(2)]
        b_mT = sch.bufs_n(2)
        PB = c.PB
        ps = c.ps
        for (dstw, srcw, nk) in ((Wua, c.w_up_a[l], 4), (Wub, c.w_up_b[l], 4), (Wo, c.w_o[l], 8)):
            sv = srcw.rearrange("(k p) n -> p k n", p=128)
            for kc in range(nk):
                op_dma(sch, "pool", dstw[:, kc, :], sv[:, kc, :], [], [b_Wo if nk == 8 else b_Wu])
        if W2pre is not None:
            b_pre = sch.buf()
            s2 = c.w_ff2[l].rearrange("(k p) n -> p k n", p=128)
            for q in range(8):
                op_dma(sch, "pool", W2pre[:, q * 4:(q + 1) * 4, :], s2[:, q * 4:(q + 1) * 4, :], [], [b_pre])

        def loads(t):
            s3 = t % 3
            rows = slice(t * 128, (t + 1) * 128)
            op_dma(sch, "sp", yt[s3][:, 0:512], c.ya[rows, :], [], [b_yt[s3]])
            op_dma(sch, "sp", yt[s3][:, 512:1024], c.yb[rows, :], [], [b_yt[s3]])
            op_dma(sch, "sp", gt[s3][:], c.gates[rows, :], [], [b_gt[s3]])
            op_dma(sch, "sp", xt[s3][:], xsrc[rows, :], [], [b_xt[s3]])

        def stage_a(t):
            s = t % 2
            s3 = t % 3
            tp = ps[:, 0:512].bitcast(BF16)
            for kc in range(8):
                op_tr(sch, tp[:, kc * 128:(kc + 1) * 128], yt[s3][:, kc * 128:(kc + 1) * 128], c.ident[:],
                      [b_yt[s3], c.b_const], [PB[0]])
            op_copy(sch, "act", yT[s][:], tp.rearrange("p (k t) -> p k t", k=8), [PB[0]], [b_yT[s]])
            for (W_, koff, bank0) in ((Wua, 0, 1), (Wub, 4, 3)):
                for sl in range(2):
                    p_ap = ps[:, (bank0 + sl) * 512:(bank0 + sl + 1) * 512]
                    for kc in range(4):
                        op_mm(sch, p_ap, yT[s][:, koff + kc, :], W_[:, kc, sl * 512:(sl + 1) * 512], kc == 0, kc == 3,
                              [b_yT[s], b_Wu], [PB[bank0 + sl]])
            op_tt(sch, "dve", m1[s][:], ps[:, 512:1536], gt[s3][:, 0:1024], ALU.mult, [PB[1], PB[2], b_gt[s3]],
                  [b_m1[s]])
            op_tt(sch, "dve", m2[s][:], ps[:, 1536:2560], gt[s3][:, 1024:2048], ALU.mult, [PB[3], PB[4], b_gt[s3]],
                  [b_m2[s]])
            op_tt(sch, "pool", mg[s][:], m1[s][:], m2[s][:], ALU.add, [b_m1[s], b_m2[s]], [b_mg[s]])

        def stage_b(t):
            s = t % 2
            s3 = t % 3
            rows = slice(t * 128, (t + 1) * 128)
            tp2 = ps[:, 5 * 512:6 * 512].bitcast(BF16)
            for kc in range(8):
                op_tr(sch, tp2[:, kc * 128:(kc + 1) * 128], mg[s][:, kc * 128:(kc + 1) * 128], c.ident[:],
                      [b_mg[s], c.b_const], [PB[5]])
            op_copy(sch, "act", mT[s][:], tp2.rearrange("p (k t) -> p k t", k=8), [PB[5]], [b_mT[s]])
            for sl in range(2):
                p_ap = ps[:, (6 + sl) * 512:(7 + sl) * 512]
                for kc in range(8):
                    op_mm(sch, p_ap, mT[s][:, kc, :], Wo[:, kc, sl * 512:(sl + 1) * 512], kc == 0, kc == 7,
                          [b_mT[s], b_Wo], [PB[6 + sl]])
            op_tt(sch, "dve", xt[s3][:], xt[s3][:], ps[:, 6 * 512:8 * 512], ALU.add, [b_xt[s3], PB[6], PB[7]],
                  [b_xt[s3]])
            op_dma(sch, "sp", c.xs[rows, :], xt[s3][:], [b_xt[s3]], [])

        loads(0)
        for t in range(NT + 1):
            if t + 1 < NT:
                loads(t + 1)
            if t < NT:
                stage_a(t)
            if t >= 1:
                stage_b(t - 1)
        sch.flush()


def phase4b(nc, sch, c, l, last, W2pre=None):
    G = 256
    with ExitStack() as es:
        def sb(name, shape, dt):
            return es.enter_context(nc.sbuf_tensor(f"p4b_{l}_{name}", shape, dt))

        W1 = sb("W1", [128, 8, DFF], BF16)
        W2 = W2pre if W2pre is not None else sb("W2", [128, 32, D], BF16)
        b_W1 = sch.bufs_n(8)
        b_W2 = sch.buf()
        gb = sb("gb", [128, D], F32)
        gf = sb("gf", [128, D], F32)
        b_gb = sch.buf()
        xt = [sb(f"xt{i}", [128, D], F32) for i in range(4)]
        b_xt = sch.bufs_n(4)
        junk = sb("junk", [128, D], BF16)
        b_junk = sch.buf()
        scr = [sb(f"scr{i}", [128, 4], F32) for i in range(2)]
        b_scr = sch.bufs_n(2)
        hb = [sb(f"hb{i}", [128, D], BF16) for i in range(2)]
        b_hb = sch.bufs_n(2)
        hT = [sb(f"hT{i}", [128, 8, G], BF16) for i in range(2)]
        b_hT = sch.bufs_n(2)
        rr = [sb(f"rr{i}", [128, G], F32) for i in range(2)]
        b_rr = sch.bufs_n(2)
        uT = sb("uT", [128, 32, G], BF16)
        b_uT = sch.buf()
        ot = [sb(f"ot{i}", [128, D], F32) for i in range(2)]
        b_ot = sch.bufs_n(2)
        PB = c.PB
        ps = c.ps
        s1 = c.w_ff1[l].rearrange("(k p) n -> p k n", p=128)
        for blk in range(8):
            op_dma(sch, "pool", W1[:, :, blk * 512:(blk + 1) * 512], s1[:, :, blk * 512:(blk + 1) * 512], [],
                   [b_W1[blk]])
        if W2pre is None:
            s2 = c.w_ff2[l].rearrange("(k p) n -> p k n", p=128)
            for q in range(8):
                op_dma(sch, "pool", W2[:, q * 4:(q + 1) * 4, :], s2[:, q * 4:(q + 1) * 4, :], [], [b_W2])
        load_bcast_vec(sch, gb[:], b_gb, c.g_mlp[l])
        if last:
            load_bcast_vec(sch, gf[:], b_gb, c.g_final)
        ntg = G // 128
        tcnt = [0]
        fcnt = [0]
        ocnt = [0]
        for g in range(S // G):
            hTg = hT[g % 2]
            bhT = b_hT[g % 2]
            xts = []
            for j in range(ntg):
                t = g * ntg + j
                k = tcnt[0]
                tcnt[0] += 1
                x_t = xt[k % 4]
                bx = b_xt[k % 4]
                xts.append((x_t, bx, t))
                sc = scr[k % 2]
                bsc = b_scr[k % 2]
                h_t = hb[k % 2]
                bh = b_hb[k % 2]
                op_dma(sch, "sp", x_t[:], c.xs[t * 128:(t + 1) * 128, :], [], [bx])
                rms_rstd(sch, c, x_t[:], bx, junk[:], b_junk, sc[:], bsc)
                op_stt(sch, h_t[:], x_t[:], sc[:, 2:3], gb[:], ALU.mult, ALU.mult, [bx, bsc, b_gb], [bh])
                tp = ps[:, 0:512].bitcast(BF16)
                for kc in range(8):
                    op_tr(sch, tp[:, kc * 128:(kc + 1) * 128], h_t[:, kc * 128:(kc + 1) * 128], c.ident[:],
                          [bh, c.b_const], [PB[0]])
                op_copy(sch, "act", hTg[:, :, j * 128:(j + 1) * 128], tp.rearrange("p (k t) -> p k t", k=8), [PB[0]],
                        [bhT])
            for fc in range(32):
                k = fcnt[0]
                fcnt[0] += 1
                bank = 1 + (k % 3)
                p_ap = ps[:, bank * 512:bank * 512 + G]
                for kc in range(8):
                    op_mm(sch, p_ap, W1[:, kc, fc * 128:(fc + 1) * 128], hTg[:, kc, :], kc == 0, kc == 7,
                          [b_W1[fc // 4], bhT], [PB[bank]])
                r = rr[k % 2]
                op_act(sch, r[:], p_ap, AF.Relu, [PB[bank]], [b_rr[k % 2]])
                op_tt(sch, "pool" if (k % 2) else "dve", uT[:, fc, :], r[:], r[:], ALU.mult, [b_rr[k % 2]], [b_uT])
            for j in range(ntg):
                x_t, bx, t = xts[j]
                k = ocnt[0]
                ocnt[0] += 1
                bank0 = 4 + 2 * (k % 2)
                for sl in range(2):
                    p_ap = ps[:, (bank0 + sl) * 512:(bank0 + sl + 1) * 512]
                    for fc in range(32):
                        op_mm(sch, p_ap, uT[:, fc, j * 128:(j + 1) * 128], W2[:, fc, sl * 512:(sl + 1) * 512],
                              fc == 0, fc == 31, [b_uT, b_W2], [PB[bank0 + sl]])
                op_tt(sch, "dve", x_t[:], x_t[:], ps[:, bank0 * 512:(bank0 + 2) * 512], ALU.add,
                      [bx, PB[bank0], PB[bank0 + 1]], [bx])
                rows = slice(t * 128, (t + 1) * 128)
                if not last:
                    op_dma(sch, "sp", c.xs[rows, :], x_t[:], [bx], [])
                else:
                    sc = scr[k % 2]
                    bsc = b_scr[k % 2]
                    o_t = ot[k % 2]
                    bo = b_ot[k % 2]
                    rms_rstd(sch, c, x_t[:], bx, junk[:], b_junk, sc[:], bsc)
                    op_stt(sch, o_t[:], x_t[:], sc[:, 2:3], gf[:], ALU.mult, ALU.mult, [bx, bsc, b_gb], [bo])
                    op_dma(sch, "sp", c.out[rows, :], o_t[:], [bo], [])
        sch.flush()


def build_nc(depth=DEPTH):
    nc = bass.Bass("TRN2", target_bir_lowering=False)
    c = Ctx()

    def din(name, shape):
        return nc.dram_tensor(name, shape, F32, kind="ExternalInput").ap()

    c.x = din("x", [S, D])
    c.g_mix = din("g_mix", [DEPTH, D])
    c.w_in = din("w_in", [DEPTH, D, N_IN])
    c.w_up_a = din("w_up_a", [DEPTH, 512, D])
    c.w_up_b = din("w_up_b", [DEPTH, 512, D])
    c.w_o = din("w_o", [DEPTH, D, D])
    c.g_mlp = din("g_mlp", [DEPTH, D])
    c.w_ff1 = din("w_ff1", [DEPTH, D, DFF])
    c.w_ff2 = din("w_ff2", [DEPTH, DFF, D])
    c.g_final = din("g_final", [D])
    k_ident = din("k_ident", [128, 128])
    k_bigident = din("k_bigident", [128, 128])
    k_maskA = din("k_maskA", [128, 128])
    k_mbdiag = din("k_mbdiag", [128, 128])
    k_initdiag = din("k_initdiag", [128, 1024])
    k_pow2 = din("k_pow2", [128, KBIS])
    c.k_selw = din("k_selw", [40, 2, 128])
    c.c_C64 = din("k_C64", [128, S])
    c.c_S64 = din("k_S64", [128, S])
    c.c_C32 = din("k_C32", [128, S])
    c.c_S32 = din("k_S32", [128, S])
    c.out = nc.dram_tensor("out", [S, D], F32, kind="ExternalOutput").ap()

    dump = DBG["dump"]

    def scratch(name, shape, dt):
        return nc.dram_tensor(name, shape, dt, kind=("ExternalOutput" if dump else "Internal")).ap()

    c.xs = scratch("xs", [S, D], F32)
    c.qaT = scratch("qaT", [512, S], BF16)
    c.kaT = scratch("kaT", [512, S], BF16)
    c.va = scratch("va", [S, 512], BF16)
    c.qbT = scratch("qbT", [512, S], BF16)
    c.kbT = scratch("kbT", [128, S], BF16)
    c.vb = scratch("vb", [S, 128], BF16)
    c.qiT = scratch("qiT", [256, S], F32)
    c.kiT = scratch("kiT", [32, S], F32)
    c.wT = scratch("wT", [8, S], F32)
    c.qsS = scratch("qsS", [16, 64, S], BF16)
    c.kiS = scratch("kiS", [64, S], BF16)
    c.gates = scratch("gates", [S, 2048], BF16)
    c.ya = scratch("ya", [S, 512], BF16)
    c.yb = scratch("yb", [S, 512], BF16)

    with ExitStack() as es:
        def sbg(name, shape, dt):
            return es.enter_context(nc.sbuf_tensor(name, shape, dt))

        c.ident = sbg("ident", [128, 128], BF16)
        c.bigident = sbg("bigident", [128, 128], BF16)
        c.maskA = sbg("maskA", [128, 128], BF16)
        c.mbdiag = sbg("mbdiag", [128, 128], BF16)
        c.initdiag = sbg("initdiag", [128, 1024], F32)
        c.zeros = sbg("zeros", [128, 1024], F32)
        c.pow2 = sbg("pow2", [128, KBIS], F32)
        eps_t = sbg("eps", [128, 1], F32)
        c.eps_ap = eps_t[:, 0:1]
        c.ps = es.enter_context(nc.psum_tensor("ps", [128, 4096], F32))
        block = es.enter_context(nc.Block())
        sch = Sched(nc, block)
        c.PB = sch.bufs_n(8, "psb")
        c.b_const = sch.buf("const")
        for (dst, src) in ((c.ident, k_ident), (c.bigident, k_bigident), (c.maskA, k_maskA), (c.mbdiag, k_mbdiag)):
            op_dma(sch, "pool", dst[:], src, [], [c.b_const])
        op_dma(sch, "sp", c.initdiag[:], k_initdiag, [], [c.b_const])
        op_dma(sch, "sp", c.pow2[:], k_pow2, [], [c.b_const])
        op_memset(sch, "dve", c.zeros[:], 0.0, [c.b_const])
        op_memset(sch, "dve", eps_t[:], EPS, [c.b_const])
        sch.flush()
        stop = DBG["stop_after"]
        done = False
        for l in range(depth):
            xsrc = c.x if l == 0 else c.xs
            for (nm, fn) in (("p1", lambda: phase1(nc, sch, c, l, xsrc)),
                             ("p2", lambda: phase2(nc, sch, c, l)),
                             ("p3", lambda: phase3(nc, sch, c, l))):
                fn()
                if stop == (l, nm):
                    done = True
                    break
            if done:
                break
            with nc.sbuf_tensor(f"W2pre_{l}", [128, 32, D], BF16) as W2pre:
                phase4a(nc, sch, c, l, xsrc, W2pre)
                if stop == (l, "p4a"):
                    done = True
                else:
                    phase4b(nc, sch, c, l, l == depth - 1, W2pre)
                    if stop == (l, "p4b"):
                        done = True
            if done:
                break
    return nc


def make_consts():
    k = {}
    k["k_ident"] = np.eye(128, dtype=np.float32)
    k["k_bigident"] = (30000.0 * np.eye(128)).astype(np.float32)
    p = np.arange(128)[:, None]
    r = np.arange(128)[None, :]
    k["k_maskA"] = np.where(r < p, 0.0, -30000.0).astype(np.float32)
    adm = (p >= 64) | (r < 64)
    k["k_mbdiag"] = np.where(adm, 1.0, 0.0).astype(np.float32)
    init = np.zeros((128, 1024), np.float32)
    init[:, 896:] = np.where(adm, 0.0, -1e30)
    k["k_initdiag"] = init
    sel = np.zeros((40, 2, 128), np.float32)
    for cc in range(2):
        for hh in range(4):
            sel[32 + 4 * cc + hh, cc, hh * 32:(hh + 1) * 32] = 1.0
    k["k_selw"] = sel
    k["k_pow2"] = np.tile((2.0 ** -(np.arange(KBIS) + 1.0))[None, :], (128, 1)).astype(np.float32)
    t = np.arange(S, dtype=np.float32)

    def tables(rot, hd):
        inv = (np.float32(THETA) ** (-(np.arange(0, rot, 2, dtype=np.float32) / np.float32(rot)))).astype(np.float32)
        ang = (t[:, None] * inv[None, :]).astype(np.float32)
        cs, sn = np.cos(ang).astype(np.float32), np.sin(ang).astype(np.float32)
        C = np.ones((128, S), np.float32)
        Sn = np.zeros((128, S), np.float32)
        for pp in range(128):
            i = pp % hd
            if i < rot:
                C[pp] = cs[:, i % (rot // 2)]
                Sn[pp] = sn[:, i % (rot // 2)]
        return C, Sn

    k["k_C64"], k["k_S64"] = tables(16, 64)
    k["k_C32"], k["k_S32"] = tables(8, 32)
    return k


def kernel(x, g_mix, w_in, w_up_a, w_up_b, w_o, g_mlp, w_ff1, w_ff2, g_final):
    f = lambda a: np.ascontiguousarray(np.asarray(a, dtype=np.float32))
    x = f(x)
    shared = dict(g_mix=f(g_mix), w_in=f(w_in), w_up_a=f(w_up_a), w_up_b=f(w_up_b), w_o=f(w_o), g_mlp=f(g_mlp),
                  w_ff1=f(w_ff1), w_ff2=f(w_ff2), g_final=f(g_final))
    shared.update(make_consts())
    nc = build_nc()
    in_maps = [dict(shared, x=x[b]) for b in range(8)]
    res = run_bass_kernel_spmd(nc, in_maps, core_ids=list(range(8)))
    return np.stack([np.asarray(res.results[b]["out"], dtype=np.float32) for b in range(8)], axis=0)
```
